# Optimizing a Trainium2 kernel written in Bass

```python
import math
import jax, jax.numpy as jnp
from jax import lax
import numpy as np

D_MODEL = 2048
BATCH = 2
SEQ = 4096
DEPTH = 1
DEC_BATCH = 4
DEC_SEQ = 8192
PAST_LEN = 128

HEAD_DIM = 128
N_HEADS_TOTAL = D_MODEL // HEAD_DIM
A_HEADS = N_HEADS_TOTAL // 2
A_KV_HEADS = 2
A_GROUP = A_HEADS // A_KV_HEADS
WINDOW = 128
BLOCK = 128
B_HEADS = N_HEADS_TOTAL - A_HEADS
B_QK_DIM = HEAD_DIM // 2
B_V_DIM = HEAD_DIM
MIX_WIDTH = A_HEADS * HEAD_DIM + B_HEADS * B_V_DIM
D_FF = 4 * D_MODEL
EPS = 1e-5
A_Q_W = A_HEADS * HEAD_DIM
A_KV_W = A_KV_HEADS * HEAD_DIM
B_QK_W = B_HEADS * 2 * B_QK_DIM
B_V_W = B_HEADS * B_V_DIM
IN_WIDTH = A_Q_W + 2 * A_KV_W + 2 * B_QK_W + B_V_W
SPLITS = [A_Q_W, A_Q_W + A_KV_W, A_Q_W + 2 * A_KV_W,
          A_Q_W + 2 * A_KV_W + B_QK_W, A_Q_W + 2 * A_KV_W + 2 * B_QK_W]

kernel_name = "hymba_window_sink_diffattn_alibi_encoder"


def _rmsnorm(x, g):
    x32 = x.astype(jnp.float32)
    y = x32 * lax.rsqrt(jnp.mean(x32 * x32, axis=-1, keepdims=True) + EPS)
    return (y * g.astype(jnp.float32)).astype(x.dtype)


def _alibi_slopes():
    i = jnp.arange(1, N_HEADS_TOTAL + 1, dtype=jnp.float32)
    s = jnp.exp2(-8.0 / N_HEADS_TOTAL * i)
    return s[0::2], s[1::2]


def _band(t):
    b, s, h, d = t.shape
    tb = t.reshape(b, s // BLOCK, BLOCK, h, d)
    tp = jnp.pad(tb, ((0, 0), (1, 1), (0, 0), (0, 0), (0, 0)))
    return jnp.concatenate([tp[:, :-2], tp[:, 1:-1], tp[:, 2:]], axis=2)


def _window_gqa_sink(q, k, v, sink, slopes):
    b, s = q.shape[0], q.shape[1]
    nb = s // BLOCK
    qb = q.reshape(b, nb, BLOCK, A_KV_HEADS, A_GROUP, HEAD_DIM)
    kb, vb = _band(k), _band(v)
    scale = 1.0 / math.sqrt(HEAD_DIM)
    sc = jnp.einsum('bnqhgd,bnkhd->bnhgqk', qb, kb).astype(jnp.float32) * scale
    qpos = jnp.arange(s, dtype=jnp.int32).reshape(nb, BLOCK)
    kpos = (jnp.arange(nb, dtype=jnp.int32)[:, None] - 1) * BLOCK + jnp.arange(3 * BLOCK, dtype=jnp.int32)[None]
    dist = jnp.abs(qpos[:, :, None] - kpos[:, None, :])
    valid = (dist <= WINDOW) & (kpos >= 0)[:, None, :] & (kpos < s)[:, None, :]
    bias = -slopes.reshape(A_KV_HEADS, A_GROUP)[None, :, :, None, None] * dist.astype(jnp.float32)[:, None, None]
    sc = jnp.where(valid[:, None, None], sc + bias, -jnp.inf)
    snk = sink.astype(jnp.float32).reshape(A_KV_HEADS, A_GROUP, 1, 1)
    m = jnp.maximum(jnp.max(sc, axis=-1, keepdims=True), snk)
    p = jnp.exp(sc - m)
    p = p / (jnp.sum(p, axis=-1, keepdims=True) + jnp.exp(snk - m))
    o = jnp.einsum('bnhgqk,bnkhd->bnqhgd', p.astype(v.dtype), vb)
    return o.reshape(b, s, A_HEADS * HEAD_DIM)


def _diff_attention(q, k, v, lam, slopes, subln_g, lambda_init):
    b, s = q.shape[0], q.shape[1]
    nb = s // BLOCK
    qb = q.reshape(b, nb, BLOCK, B_HEADS, 2, B_QK_DIM).transpose(1, 0, 3, 4, 2, 5)
    kt = k.transpose(0, 2, 3, 1, 4)
    vt = v.transpose(0, 2, 1, 3)
    kpos = jnp.arange(s, dtype=jnp.int32)
    scale = 1.0 / math.sqrt(B_QK_DIM)
    sl = slopes[:, None, None, None]

    def block(args):
        qblk, q0 = args
        sc = jnp.einsum('bhiqd,bhikd->bhiqk', qblk, kt).astype(jnp.float32) * scale
        qpos = q0 + jnp.arange(BLOCK, dtype=jnp.int32)
        dist = jnp.abs(qpos[:, None] - kpos[None, :]).astype(jnp.float32)
        p = jax.nn.softmax(sc - sl * dist, axis=-1)
        a = p[:, :, 0] - lam * p[:, :, 1]
        return jnp.einsum('bhqk,bhkd->bhqd', a.astype(vt.dtype), vt)

    starts = jnp.arange(nb, dtype=jnp.int32) * BLOCK
    o = lax.map(block, (qb, starts))
    o = o.transpose(1, 0, 3, 2, 4).reshape(b, s, B_HEADS, B_V_DIM)
    o = (_rmsnorm(o, subln_g) * (1.0 - lambda_init)).astype(v.dtype)
    return o.reshape(b, s, B_HEADS * B_V_DIM)


def _encoder(x, norm_attn_g, w_in, sink_logits, lambda_q1, lambda_k1, lambda_q2, lambda_k2,
             diff_subln_g, w_out, norm_mlp_g, w_up, w_down, norm_final_g):
    b, s = x.shape[0], x.shape[1]
    slopes_a, slopes_b = _alibi_slopes()
    for l in range(DEPTH):
        h = _rmsnorm(x, norm_attn_g[l])
        proj = h @ w_in[l]
        aq, ak, av, bq, bk, bv = jnp.split(proj, SPLITS, axis=-1)
        oa = _window_gqa_sink(aq.reshape(b, s, A_HEADS, HEAD_DIM),
                              ak.reshape(b, s, A_KV_HEADS, HEAD_DIM),
                              av.reshape(b, s, A_KV_HEADS, HEAD_DIM),
                              sink_logits[l], slopes_a)
        lambda_init = 0.8 - 0.6 * math.exp(-0.3 * l)
        lam = (jnp.exp(jnp.sum(lambda_q1[l].astype(jnp.float32) * lambda_k1[l].astype(jnp.float32)))
               - jnp.exp(jnp.sum(lambda_q2[l].astype(jnp.float32) * lambda_k2[l].astype(jnp.float32)))
               + lambda_init)
        ob = _diff_attention(bq.reshape(b, s, B_HEADS, 2, B_QK_DIM),
                             bk.reshape(b, s, B_HEADS, 2, B_QK_DIM),
                             bv.reshape(b, s, B_HEADS, B_V_DIM),
                             lam, slopes_b, diff_subln_g[l], lambda_init)
        x = x + jnp.concatenate([oa, ob], axis=-1) @ w_out[l]
        h = _rmsnorm(x, norm_mlp_g[l])
        u = jax.nn.relu(h @ w_up[l])
        x = x + (u * u) @ w_down[l]
    return _rmsnorm(x, norm_final_g)


def setup_inputs(seed: int = 0) -> dict:
    key = jax.random.key(seed)
    ks = jax.random.split(key, 16)
    f32 = jnp.float32
    nrm = lambda k, shape, sc: jax.random.normal(k, shape, f32) * sc
    return {
        "x_prompt": nrm(ks[0], (BATCH, SEQ, D_MODEL), 1.0),
        "x_sample": nrm(ks[1], (DEC_BATCH, DEC_SEQ, D_MODEL), 1.0),
        "norm_attn_g": 1.0 + nrm(ks[2], (DEPTH, D_MODEL), 0.02),
        "w_in": nrm(ks[3], (DEPTH, D_MODEL, IN_WIDTH), D_MODEL ** -0.5),
        "sink_logits": nrm(ks[4], (DEPTH, A_HEADS), 0.5),
        "lambda_q1": nrm(ks[5], (DEPTH, B_QK_DIM), 0.1),
        "lambda_k1": nrm(ks[6], (DEPTH, B_QK_DIM), 0.1),
        "lambda_q2": nrm(ks[7], (DEPTH, B_QK_DIM), 0.1),
        "lambda_k2": nrm(ks[8], (DEPTH, B_QK_DIM), 0.1),
        "diff_subln_g": 1.0 + nrm(ks[9], (DEPTH, B_V_DIM), 0.02),
        "w_out": nrm(ks[10], (DEPTH, MIX_WIDTH, D_MODEL), MIX_WIDTH ** -0.5),
        "norm_mlp_g": 1.0 + nrm(ks[11], (DEPTH, D_MODEL), 0.02),
        "w_up": nrm(ks[12], (DEPTH, D_MODEL, D_FF), D_MODEL ** -0.5),
        "w_down": nrm(ks[13], (DEPTH, D_FF, D_MODEL), D_FF ** -0.5),
        "norm_final_g": 1.0 + nrm(ks[14], (D_MODEL,), 0.02),
    }


def reference(x_prompt, x_sample, norm_attn_g, w_in, sink_logits, lambda_q1, lambda_k1,
              lambda_q2, lambda_k2, diff_subln_g, w_out, norm_mlp_g, w_up, w_down, norm_final_g):
    y_prompt = _encoder(x_prompt, norm_attn_g, w_in, sink_logits, lambda_q1, lambda_k1, lambda_q2,
                        lambda_k2, diff_subln_g, w_out, norm_mlp_g, w_up, w_down, norm_final_g)
    y_sample = _encoder(x_sample, norm_attn_g, w_in, sink_logits, lambda_q1, lambda_k1, lambda_q2,
                        lambda_k2, diff_subln_g, w_out, norm_mlp_g, w_up, w_down, norm_final_g)
    return (y_prompt, y_sample)
```

```python
import math
import numpy as np
import concourse.bass as bass
import concourse.mybir as mybir
from concourse.bass_utils import run_bass_kernel_spmd

F32 = mybir.dt.float32
BF16 = mybir.dt.bfloat16
AF = mybir.ActivationFunctionType
ALU = mybir.AluOpType

D = 2048
NCH = 16
INW = 4608
DFF = 8192
NTOK = 12288
NOWN = 5120
NEG = -30000.0
EPS = 1e-5
LAMBDA_INIT = 0.8 - 0.6 * math.exp(0.0)
ALIBI_SKIP_T = None
JOBS = [dict(tok0=0, n_own=4096, n_tot=8192, own0=0), dict(tok0=8192, n_own=1024, n_tot=4096, own0=4096)]
OWN_TILES = list(range(8)) + [16, 17]


class _Op:
    __slots__ = ("eng", "fn", "deps", "is_dma", "sig", "sem", "val", "ring_wait", "bar")

    def __init__(self, eng, fn, is_dma):
        self.eng = eng
        self.fn = fn
        self.deps = []
        self.is_dma = is_dma
        self.sig = False
        self.sem = None
        self.val = 0
        self.ring_wait = None
        self.bar = None


class _Rec:
    def __getattr__(self, name):
        def f(*a, **k):
            self.call = (name, a, k)
        return f


class Prog:
    ENGS = ("pe", "act", "dve", "pool", "sp")
    DMAQ = ("sp", "pool", "act")
    RING = 8

    def __init__(self, nc):
        self.nc = nc
        self.ops = []
        self.last_writer = {}
        self.readers = {}
        self.last_op = {}
        self.arena_base = None

    def arena_init(self):
        self.arena_base = self.nc.sbuf_base
        self.arena_top = self.nc.sbuf_top
        self.arena_ptr = self.arena_base
        self._an = 0

    def arena_reset(self):
        self.arena_ptr = self.arena_base

    def asb(self, name, shape, dtype):
        nbytes = int(np.prod(shape[1:])) * (2 if dtype == BF16 else 4)
        off = (self.arena_ptr + 63) // 64 * 64
        assert off + nbytes <= self.arena_top, ("SBUF arena overflow", name, off + nbytes - self.arena_top)
        self.arena_ptr = off + nbytes
        self._an += 1
        return self.nc.alloc_sbuf_tensor_at("%s_%d" % (name, self._an), list(shape), dtype, offset=off)

    def sb(self, name, shape, dtype):
        return self.nc.alloc_sbuf_tensor(name, list(shape), dtype)

    def _record(self, op, reads, writes):
        deps = {}
        for r in reads:
            w = self.last_writer.get(r)
            if w is not None:
                deps[id(w)] = (w, True)
        for wk in writes:
            w = self.last_writer.get(wk)
            if w is not None and id(w) not in deps:
                deps[id(w)] = (w, False)
            rd = self.readers.get(wk)
            if rd:
                for o in rd[0].values():
                    if id(o) not in deps:
                        deps[id(o)] = (o, False)
                for o in rd[1]:
                    if id(o) not in deps:
                        deps[id(o)] = (o, False)
        for d, raw in deps.values():
            if d is op:
                continue
            if d.eng == op.eng and not d.is_dma and not op.is_dma:
                if op.eng == "pe" or not raw:
                    continue
            op.deps.append(d)
            d.sig = True
        for r in reads:
            rd = self.readers.get(r)
            if rd is None:
                rd = self.readers[r] = ({}, [])
            if op.is_dma:
                rd[1].append(op)
            else:
                rd[0][op.eng] = op
        for wk in writes:
            self.last_writer[wk] = op
            self.readers[wk] = ({}, [])
        if not op.is_dma:
            self.last_op[op.eng] = op
        self.ops.append(op)
        return op

    def op(self, eng, fn, reads=(), writes=()):
        r = _Rec()
        fn(r)
        name, a, k = r.call
        return self._record(_Op(eng, lambda e: getattr(e, name)(*a, **k), False), reads, writes)

    def i(self, eng, name, *args, reads=(), writes=(), **kw):
        return self._record(_Op(eng, lambda e: getattr(e, name)(*args, **kw), False), reads, writes)

    def dma(self, eng, out, in_, reads=(), writes=()):
        o = _Op(eng, lambda e: e.dma_start(out=out, in_=in_), True)
        o.sig = True
        return self._record(o, reads, writes)

    def barrier(self):
        b = _Op(None, None, False)
        b.bar = {}
        for e, o in self.last_op.items():
            o.sig = True
        b.deps = list(self.last_op.values())
        self.ops.append(b)
        self.last_writer = {}
        self.readers = {}
        self.last_op = {}

    def emit(self):
        nc = self.nc
        sems = {e: nc.alloc_semaphore("s_" + e) for e in self.ENGS}
        rings = {e: [nc.alloc_semaphore("r_%s%d" % (e, i)) for i in range(self.RING)] for e in self.DMAQ}
        cnt = {e: 0 for e in self.ENGS}
        dcnt = {e: 0 for e in rings}
        ringval = {}
        per_eng = {e: [] for e in self.ENGS}
        for o in self.ops:
            if o.bar is not None:
                o.bar = dict(ringval)
                for e in self.ENGS:
                    per_eng[e].append(o)
                continue
            if o.is_dma:
                n = dcnt[o.eng]
                dcnt[o.eng] += 1
                o.sem = rings[o.eng][n % self.RING]
                o.val = 16 * (n // self.RING + 1)
                ringval[id(o.sem)] = (o.sem, o.val)
                if n >= self.RING:
                    o.ring_wait = (o.sem, 16 * (n // self.RING))
            elif o.sig:
                cnt[o.eng] += 1
                o.sem = sems[o.eng]
                o.val = cnt[o.eng]
            per_eng[o.eng].append(o)
        self.stats = dict(cnt=dict(cnt), dcnt=dict(dcnt), nops={e: len(v) for e, v in per_eng.items()})
        final_rings = dict(ringval)

        def run(eng_name, e):
            waited = {}

            def wait(sem, val):
                k = id(sem)
                if waited.get(k, 0) >= val:
                    return
                waited[k] = val
                e.wait_ge(sem, val)

            for o in per_eng[eng_name]:
                if o.bar is not None:
                    for d in o.deps:
                        if d.eng != eng_name:
                            wait(d.sem, d.val)
                    for sem, val in o.bar.values():
                        wait(sem, val)
                    continue
                for d in o.deps:
                    wait(d.sem, d.val)
                if o.ring_wait is not None:
                    wait(*o.ring_wait)
                ins = o.fn(e)
                if o.sig:
                    ins.then_inc(o.sem, 16 if o.is_dma else 1)
            if eng_name == "sp":
                for sem, val in final_rings.values():
                    wait(sem, val)

        with nc.Block() as block:
            @block.tensor
            def _(e):
                run("pe", e)

            @block.scalar
            def _(e):
                run("act", e)

            @block.vector
            def _(e):
                run("dve", e)

            @block.gpsimd
            def _(e):
                run("pool", e)

            @block.sync
            def _(e):
                run("sp", e)


class RR:
    def __init__(self, items):
        self.items = list(items)
        self.i = 0

    def next(self):
        v = self.items[self.i % len(self.items)]
        self.i += 1
        return v


def build(phases=("cast", "p1", "p2", "p3", "p4"), dbg=False):
    nc = bass.Bass("TRN2", target_bir_lowering=False)
    P = Prog(nc)
    lp = nc.allow_low_precision("bf16 matmul operands, fp32 accumulation")
    lp.__enter__()
    ncd = nc.allow_non_contiguous_dma("tiny parameter-vector loads")
    ncd.__enter__()

    def din(name, shape):
        return nc.dram_tensor(name, list(shape), F32, kind="ExternalInput").ap()

    def dscr(name, shape, dt=BF16):
        kind = "ExternalOutput" if dbg else "Internal"
        return nc.dram_tensor(name, list(shape), dt, kind=kind).ap()

    xc = din("xc", [NTOK, D])
    w_in = din("w_in", [D, INW])
    w_out = din("w_out", [D, D])
    w_up = din("w_up", [D, DFF])
    w_down = din("w_down", [DFF, D])
    g_attn = din("g_attn", [1, D])
    g_mlp = din("g_mlp", [1, D])
    g_fin = din("g_fin", [1, D])
    sink = din("sink", [1, 8])
    lq1 = din("lq1", [1, 64])
    lk1 = din("lk1", [1, 64])
    lq2 = din("lq2", [1, 64])
    lk2 = din("lk2", [1, 64])
    subg = din("subg", [1, 128])
    qaugP = din("qaugP", [8, NOWN])
    qaugM = din("qaugM", [8, NOWN])
    kaug8 = din("kaug8", [8, 8, NTOK])
    dtabB_d = din("dtabB", [128, 8 * 128])
    tabA_d = din("tabA", [128, 8 * 3 * 128])
    hmask_d = din("hmask", [128, 4])
    y = nc.dram_tensor("y", [NOWN, D], F32, kind="ExternalOutput").ap()

    w_in_bf = dscr("w_in_bf", [9, 128, NCH, 512])
    w_out_bf = dscr("w_out_bf", [4, 128, NCH, 512])
    w_up_bf = dscr("w_up_bf", [16, 128, NCH, 512])
    w_down_bf = dscr("w_down_bf", [16, 128, NCH, 512])
    qaT = dscr("qaT", [8, 128, NOWN])
    kaT = dscr("kaT", [2, 128, NTOK])
    va = dscr("va", [2, NTOK, 128])
    qbT = dscr("qbT", [8, 128, NOWN])
    kbT = dscr("kbT", [8, 128, NTOK])
    vb = dscr("vb", [8, NTOK, 128])
    attT = dscr("attT", [16, 128, NOWN])

    ident = P.sb("ident", [128, 128], BF16)
    identf = P.sb("identf", [128, 128], F32)
    ones_f = P.sb("ones_f", [128, 128], F32)
    ones_bf = P.sb("ones_bf", [128, 2], BF16)
    ones_bw = P.sb("ones_bw", [128, 128], BF16)
    sel = [P.sb("sel0", [128, 64], BF16), P.sb("sel1", [128, 64], BF16)]
    self32 = P.sb("self32", [128, 128], F32)
    epsb = P.sb("epsb", [128, 1], F32)
    tabA = P.sb("tabA_sb", [128, 8, 3, 128], F32)
    dtabB = P.sb("dtabB_sb", [128, 8, 128], F32)
    hmask = P.sb("hmask_sb", [128, 4], F32)
    esink = P.sb("esink", [1, 8], F32)
    lamv = P.sb("lamv", [128, 4, 64], F32)
    lamw = P.sb("lamw", [128, 8], F32)
    neglam = P.sb("neglam", [128, 1], F32)
    gsub = P.sb("gsub", [128, 1], F32)
    ss = P.sb("ss", [128, 4], F32)
    rstd = P.sb("rstd", [128, 4], F32)
    pairs = [nc.alloc_psum_tensor("pair%d" % i, [128, 2, 512], F32) for i in range(4)]
    banks = [pairs[i // 2][:, i % 2, :] for i in range(8)]
    P.arena_init()

    def bank_bf(i):
        return banks[i].bitcast(BF16).rearrange("p (c t) -> p c t", c=8)

    P.op("pool", lambda e: e.memset(identf[:], 0.0), writes=["identf"])
    P.op("pool", lambda e: e.affine_select(identf[:], identf[:], pattern=[[-1, 128]], compare_op=ALU.not_equal,
                                           fill=1.0, base=0, channel_multiplier=1), reads=["identf"], writes=["identf"])
    P.op("dve", lambda e: e.tensor_copy(ident[:], identf[:]), reads=["identf"], writes=["ident"])
    P.op("pool", lambda e: e.memset(ones_f[:], 1.0), writes=["ones_f"])
    P.op("pool", lambda e: e.memset(self32[:], 0.0), writes=["self32"])
    P.op("pool", lambda e: e.memset(self32[:, 0:1], 1.0), reads=["self32"], writes=["self32"])
    P.op("pool", lambda e: e.memset(self32[:, 96:97], 1.0), reads=["self32"], writes=["self32"])
    P.op("dve", lambda e: e.tensor_copy(sel[0][:], self32[:, 0:64]), reads=["self32"], writes=["sel0"])
    P.op("dve", lambda e: e.tensor_copy(sel[1][:], self32[:, 64:128]), reads=["self32"], writes=["sel1"])
    P.op("dve", lambda e: e.tensor_copy(ones_bf[:], ones_f[:, 0:2]), reads=["ones_f"], writes=["ones_bf"])
    P.op("dve", lambda e: e.tensor_copy(ones_bw[:], ones_f[:]), reads=["ones_f"], writes=["ones32"])
    ones32 = ones_bw[:, 0:32]
    P.op("pool", lambda e: e.memset(epsb[:], EPS), writes=["epsb"])
    P.dma("sp", tabA[:].rearrange("p a b c -> p (a b c)"), tabA_d, writes=["tabA"])
    P.dma("sp", dtabB[:].rearrange("p a b -> p (a b)"), dtabB_d, writes=["dtabB"])
    P.dma("sp", hmask[:], hmask_d, writes=["hmask"])
    P.dma("sp", esink[:], sink, writes=["esink"])
    P.op("act", lambda e: e.activation(esink[:], esink[:], AF.Exp), reads=["esink"], writes=["esink"])
    for i, v in enumerate((lq1, lk1, lq2, lk2)):
        P.dma("sp", lamv[:, i, :], v[0].partition_broadcast(128), writes=[("lamv", i)])
    P.dma("sp", gsub[:], subg.rearrange("o d -> d o"), writes=["gsub"])
    P.op("dve", lambda e: e.tensor_tensor(lamv[:, 0, :], lamv[:, 0, :], lamv[:, 1, :], ALU.mult),
         reads=[("lamv", 0), ("lamv", 1)], writes=[("lamv", 0)])
    P.op("dve", lambda e: e.tensor_tensor(lamv[:, 2, :], lamv[:, 2, :], lamv[:, 3, :], ALU.mult),
         reads=[("lamv", 2), ("lamv", 3)], writes=[("lamv", 2)])
    P.op("dve", lambda e: e.reduce_sum(lamw[:, 0:1], lamv[:, 0, :], mybir.AxisListType.X), reads=[("lamv", 0)], writes=["lamw0"])
    P.op("dve", lambda e: e.reduce_sum(lamw[:, 1:2], lamv[:, 2, :], mybir.AxisListType.X), reads=[("lamv", 2)], writes=["lamw1"])
    P.op("act", lambda e: e.activation(lamw[:, 2:4], lamw[:, 0:2], AF.Exp), reads=["lamw0", "lamw1"], writes=["lamw2"])
    P.op("dve", lambda e: e.tensor_tensor(lamw[:, 4:5], lamw[:, 3:4], lamw[:, 2:3], ALU.subtract), reads=["lamw2"], writes=["lamw4"])
    P.op("dve", lambda e: e.tensor_scalar(neglam[:], lamw[:, 4:5], -LAMBDA_INIT, None, ALU.add), reads=["lamw4"], writes=["neglam"])
    P.op("dve", lambda e: e.tensor_scalar(gsub[:], gsub[:], 1.0 - LAMBDA_INIT, None, ALU.mult), reads=["gsub"], writes=["gsub"])

    evac_rr = RR(["act", "dve"])

    def evac(dst, src, reads, writes, scale=None, eng=None):
        eng = eng or evac_rr.next()
        if eng == "act":
            if scale is None:
                P.op("act", lambda e: e.activation(dst, src, AF.Copy), reads=reads, writes=writes)
            else:
                P.op("act", lambda e: e.activation(dst, src, AF.Copy, scale=float(scale)), reads=reads, writes=writes)
        else:
            if scale is None:
                P.op(eng, lambda e: e.tensor_copy(dst, src), reads=reads, writes=writes)
            else:
                P.op(eng, lambda e: e.tensor_scalar(dst, src, float(scale), None, ALU.mult), reads=reads, writes=writes)

    late_casts = []
    if "cast" in phases:
        for g in [2, 5, 6, 7, 8, 0, 1, 3, 4]:
            P.dma("pool", w_in_bf[g], w_in[:, g * 512:(g + 1) * 512].rearrange("(kc p) n -> p kc n", p=128), writes=[("w_in_bf", g)])
        for g in range(4):
            late_casts.append((w_out_bf[g], w_out[:, g * 512:(g + 1) * 512].rearrange("(kc p) n -> p kc n", p=128), ("w_out_bf", g)))
        for g in range(16):
            late_casts.append((w_up_bf[g], w_up[:, g * 512:(g + 1) * 512].rearrange("(kc p) n -> p kc n", p=128), ("w_up_bf", g)))
        for g in range(4):
            for hh in range(4):
                late_casts.append((w_down_bf[g * 4 + hh],
                                   w_down[hh * 2048:(hh + 1) * 2048, g * 512:(g + 1) * 512].rearrange("(fc p) n -> p fc n", p=128),
                                   ("w_down_bf", g, hh)))

    def emit_late_cast():
        if late_casts:
            dst, src, key = late_casts.pop(0)
            P.dma("pool", dst, src, writes=[key])

    def norm_front(xsrc, xkeys, gtab, gkey, hb, hbkey, ssi, junk):
        P.op("act", lambda e: e.activation(junk[:], xsrc, AF.Square, accum_out=ss[:, ssi:ssi + 1]),
             reads=xkeys, writes=["junk", ("ss", ssi)])
        P.op("act", lambda e: e.activation(rstd[:, ssi:ssi + 1], ss[:, ssi:ssi + 1], AF.Ln, scale=1.0 / D, bias=epsb[:]),
             reads=[("ss", ssi), "epsb"], writes=[("rstd", ssi)])
        P.op("act", lambda e: e.activation(rstd[:, ssi:ssi + 1], rstd[:, ssi:ssi + 1], AF.Exp, scale=-0.5),
             reads=[("rstd", ssi)], writes=[("rstd", ssi)])
        P.op("dve", lambda e: e.scalar_tensor_tensor(out=hb[:], in0=xsrc, scalar=rstd[:, ssi:ssi + 1], in1=gtab[:],
                                                     op0=ALU.mult, op1=ALU.mult),
             reads=xkeys + [("rstd", ssi), gkey], writes=[hbkey])

    def norm_back(hb, hbkey, hTt, hTkeyf, j):
        for half in range(2):
            pt = bank_bf(half)
            for i in range(8):
                c = half * 8 + i
                P.op("pe", lambda e: e.transpose(pt[:, i, :], hb[:, c * 128:(c + 1) * 128], ident[:]),
                     reads=[hbkey, "ident"], writes=[("bank", half)])
            evac(hTt[:, half * 8:(half + 1) * 8, j * 128:(j + 1) * 128], pt, reads=[("bank", half)],
                 writes=[hTkeyf(j, half)])

    def norm_transpose(xsrc, xkeys, gtab, gkey, hb, hbkey, hTt, hTkeyf, j, ssi, junk):
        norm_front(xsrc, xkeys, gtab, gkey, hb, hbkey, ssi, junk)
        norm_back(hb, hbkey, hTt, hTkeyf, j)

    if "p1" in phases:
        P.arena_reset()
        gtab = P.asb("gtab_attn", [128, D], F32)
        xt = [P.asb("xt", [128, D], F32) for _ in range(3)]
        junk = P.asb("junk", [128, D], BF16)
        hb = [P.asb("hb", [128, D], BF16) for _ in range(2)]
        hT = [P.asb("hT", [128, NCH, 512], BF16) for _ in range(2)]
        wg = [P.asb("wg", [128, NCH, 512], BF16) for _ in range(3)]
        stage = [P.asb("stage", [128, 512], BF16) for _ in range(6)]
        P.dma("sp", gtab[:], g_attn[0].partition_broadcast(128), writes=["gtab"])
        pgi = RR([2, 3, 4, 5, 6, 7])
        sti = RR(range(6))
        sA = 1.0 / math.sqrt(128.0)
        NT1 = NTOK // 512
        subc = [0]

        def pro_front(tt, j):
            sub = tt * 4 + j
            s3 = sub % 3
            tok = tt * 512 + j * 128
            P.dma("sp", xt[s3][:], xc[tok:tok + 128, :], writes=[("xt", s3)])
            norm_front(xt[s3][:], [("xt", s3)], gtab, "gtab", hb[sub % 2], ("hb", sub % 2), sub % 4, junk)

        def pro_back(tt, j):
            sub = tt * 4 + j
            hs = tt % 2
            norm_back(hb[sub % 2], ("hb", sub % 2), hT[hs], lambda j, half: ("hT", hs, j, half), j)

        items = []
        for tt in range(NT1):
            own = tt in OWN_TILES
            for gi, g in enumerate([2, 5, 6, 7, 8, 0, 1, 3, 4] if own else [2, 5, 6, 7, 8]):
                items.append((tt, gi, g))

        def load_w(ii):
            tt, gi, g = items[ii]
            ws = ii % 3
            P.dma("sp", wg[ws][:], w_in_bf[g], reads=[("w_in_bf", g)], writes=[("wg", ws)])

        load_w(0)
        load_w(1)
        for j in range(4):
            pro_front(0, j)
            pro_back(0, j)
        for ii, (tt, gi, g) in enumerate(items):
            if ii + 2 < len(items):
                load_w(ii + 2)
            own = tt in OWN_TILES
            o0 = OWN_TILES.index(tt) * 512 if own else None
            hs = tt % 2
            hTt = hT[hs]
            ws = ii % 3
            wt = wg[ws]
            if g in (0, 1):
                fm = [(ci, qaT, g * 4 + ci, sA, True) for ci in range(4)]
            elif g == 2:
                fm = [(ci, kaT, ci, None, False) for ci in range(2)]
            elif g in (3, 4):
                fm = [(ci, qbT, (g - 3) * 4 + ci, 0.125, True) for ci in range(4)]
            elif g in (5, 6):
                fm = [(ci, kbT, (g - 5) * 4 + ci, None, False) for ci in range(4)]
            else:
                fm = []
            for (ci, dst, hidx, scale, isq) in fm:
                b = pgi.next()
                for kc in range(NCH):
                    P.op("pe", lambda e, b=b, kc=kc, ci=ci, wt=wt, hTt=hTt: e.matmul(
                        banks[b], wt[:, kc, ci * 128:(ci + 1) * 128], hTt[:, kc, :], start=(kc == 0), stop=(kc == NCH - 1)),
                        reads=[("wg", ws)] + [("hT", hs, jj, kc // 8) for jj in range(4)], writes=[("bank", b)])
                st = sti.next()
                evac(stage[st][:], banks[b], reads=[("bank", b)], writes=[("stage", st)], scale=scale)
                if isq:
                    P.dma("pool", dst[hidx, :, o0:o0 + 512], stage[st][:], reads=[("stage", st)])
                else:
                    P.dma("pool", dst[hidx, :, tt * 512:(tt + 1) * 512], stage[st][:], reads=[("stage", st)])
            if g == 2:
                tmv = (256, 256, va, 0, 2)
            elif g in (7, 8):
                tmv = (0, 512, vb, (g - 7) * 4, 4)
            else:
                tmv = None
            if tmv is not None:
                c0, ncols, dst, h0, nh = tmv
                for j in range(4):
                    b = pgi.next()
                    for kc in range(NCH):
                        P.op("pe", lambda e, b=b, kc=kc, j=j, wt=wt, hTt=hTt, c0=c0, ncols=ncols: e.matmul(
                            banks[b][:, 0:ncols], hTt[:, kc, j * 128:(j + 1) * 128], wt[:, kc, c0:c0 + ncols],
                            start=(kc == 0), stop=(kc == NCH - 1)),
                            reads=[("wg", ws), ("hT", hs, j, kc // 8)], writes=[("bank", b)])
                    st = sti.next()
                    evac(stage[st][:, 0:ncols], banks[b][:, 0:ncols], reads=[("bank", b)], writes=[("stage", st)])
                    tok = tt * 512 + j * 128
                    P.dma("pool", dst[h0:h0 + nh, tok:tok + 128, :].rearrange("h t d -> t h d"),
                          stage[st][:, 0:ncols].rearrange("p (h d) -> p h d", h=nh), reads=[("stage", st)])
            if tt + 1 < NT1:
                if gi < 4:
                    pro_front(tt + 1, gi)
                if 1 <= gi <= 4:
                    pro_back(tt + 1, gi - 1)
            if ii % 4 == 3:
                emit_late_cast()
        while late_casts:
            emit_late_cast()
        P.barrier()

    if "p2" in phases:
        P.arena_reset()
        kA = P.asb("kA", [128, 8192], BF16)
        vA = P.asb("vA", [128, 64, 128], BF16)
        qA = [P.asb("qA", [128, 4096], BF16) for _ in range(2)]
        sAb = [P.asb("sAb", [128, 3, 512], F32) for _ in range(2)]
        Eb = [P.asb("Eb", [128, 3, 512], BF16) for _ in range(2)]
        zs = P.asb("zs", [1, 512], F32)
        zh = P.asb("zh", [1, 512], BF16)
        zl = P.asb("zl", [1, 512], BF16)
        bz = P.asb("bz", [128, 512], F32)
        zrow = P.asb("zrow", [1, 512], BF16)
        stg = [P.asb("stgA", [128, 512], BF16) for _ in range(2)]
        P.op("pool", lambda e: e.memset(zrow[:], 0.0), writes=["zrow"])
        PIECES = {-1: (0, 0, 0, 1, 2), 1: (0, 128, 0, 3, 0), 0: (1, 0, 0, 2, 1), 3: (1, 256, 2, 2, 0), 2: (2, 0, 1, 3, 0), 4: (2, 384, 3, 1, 0)}
        hq = 0
        stcl = [0]
        tl = 0
        for ji, job in enumerate(JOBS):
            tok0, n_own, n_tot, own0 = job["tok0"], job["n_own"], job["n_tot"], job["own0"]
            nq = n_own // 128
            nqt = n_own // 512
            nb = n_tot // 128
            for kv in range(2):
                P.dma("sp", kA[:, 0:n_tot], kaT[kv, :, tok0:tok0 + n_tot], writes=["kA"])
                P.dma("sp", vA[:, 0:nb, :], va[kv, tok0:tok0 + n_tot, :].rearrange("(b p) d -> p b d", p=128), writes=["vA"])
                for hh in range(4):
                    h = kv * 4 + hh
                    qs = hq % 2
                    hq += 1
                    qAt = qA[qs]
                    P.dma("sp", qAt[:, 0:n_own], qaT[h, :, own0:own0 + n_own], writes=[("qA", qs)])

                    def kb_of(qt, r):
                        kb = qt * 4 + r
                        if kb < 0:
                            return nq + 1
                        if kb >= nq:
                            return nq
                        return kb

                    def emit_S(qt):
                        for r in range(-1, 5):
                            bnk, c0, qb0, nqb, j0 = PIECES[r]
                            kb = kb_of(qt, r)
                            P.op("pe", lambda e: e.matmul(banks[bnk][:, c0:c0 + nqb * 128], kA[:, kb * 128:(kb + 1) * 128],
                                                          qAt[:, (qt * 4 + qb0) * 128:(qt * 4 + qb0 + nqb) * 128], start=True, stop=True),
                                 reads=["kA", ("qA", qs)], writes=[("bank", bnk)])

                    def emit_epi(qt, ps):
                        pO = banks[3 + ps]
                        pZ = banks[5 + ps]
                        P.op("dve", lambda e: e.tensor_scalar(zs[:], pZ[0:1, :], esink[0:1, h:h + 1], None, ALU.add),
                             reads=[("bank", 5 + ps), "esink"], writes=["zs"])
                        P.op("dve", lambda e: e.tensor_copy(zh[:], zs[:]), reads=["zs"], writes=["zh"])
                        P.op("dve", lambda e: e.tensor_tensor(zl[:], zs[:], zh[:], ALU.subtract), reads=["zs", "zh"], writes=["zl"])
                        P.op("pe", lambda e: e.matmul(banks[7], ones_bw[0:1, :], zh[:], start=True, stop=False),
                             reads=["ones32", "zh"], writes=[("bank", 7)])
                        P.op("pe", lambda e: e.matmul(banks[7], ones_bw[0:1, :], zl[:], start=False, stop=True),
                             reads=["ones32", "zl"], writes=[("bank", 7)])
                        P.op("act", lambda e: e.activation(bz[:], banks[7], AF.Ln), reads=[("bank", 7)], writes=["bz"])
                        P.op("act", lambda e: e.activation(bz[:], bz[:], AF.Exp, scale=-1.0), reads=["bz"], writes=["bz"])
                        st = stcl[0] % 2
                        stcl[0] += 1
                        P.op("dve", lambda e: e.tensor_tensor(stg[st][:], pO, bz[:], ALU.mult),
                             reads=[("bank", 3 + ps), "bz"], writes=[("stgA", st)])
                        P.dma("sp", attT[h, :, own0 + qt * 512:own0 + (qt + 1) * 512], stg[st][:], reads=[("stgA", st)])

                    emit_S(0)
                    for qt in range(nqt):
                        sl = tl % 2
                        ps = tl % 2
                        tl += 1
                        sAt = sAb[sl]
                        Et = Eb[sl]
                        for r in range(-1, 5):
                            bnk, c0, qb0, nqb, j0 = PIECES[r]
                            tb = tabA[:, h, j0:j0 + nqb, :].rearrange("p j q -> p (j q)")
                            P.op("dve", lambda e: e.tensor_tensor(sAt[:, bnk, c0:c0 + nqb * 128], banks[bnk][:, c0:c0 + nqb * 128], tb, ALU.add),
                                 reads=[("bank", bnk), "tabA"], writes=[("sAb", sl, bnk)])
                            if (qt == 0 and r == -1) or (qt == nqt - 1 and r == 4):
                                mc = 2 * ji + (0 if r == -1 else 1)
                                P.op("dve", lambda e: e.tensor_scalar(sAt[:, bnk, c0:c0 + 128], sAt[:, bnk, c0:c0 + 128], hmask[:, mc:mc + 1], None, ALU.add),
                                     reads=[("sAb", sl, bnk), "hmask"], writes=[("sAb", sl, bnk)])
                        for bnk in range(3):
                            P.op("act", lambda e: e.activation(Et[:, bnk, :], sAt[:, bnk, :], AF.Exp), reads=[("sAb", sl, bnk)], writes=[("Eb", sl, bnk)])
                        if qt + 1 < nqt:
                            emit_S(qt + 1)
                        if qt > 0:
                            emit_epi(qt - 1, 1 - ps)
                        P.op("pe", lambda e: e.matmul(banks[3 + ps], zrow[0:1, 0:128], zrow[0:1, :], start=True, stop=False),
                             reads=["zrow"], writes=[("bank", 3 + ps)])
                        P.op("pe", lambda e: e.matmul(banks[5 + ps][0:1, :], zrow[0:1, 0:1], zrow[0:1, :], start=True, stop=False),
                             reads=["zrow"], writes=[("bank", 5 + ps)])
                        for r in range(-1, 5):
                            bnk, c0, qb0, nqb, j0 = PIECES[r]
                            kb = kb_of(qt, r)
                            last = (r == 4)
                            P.op("pe", lambda e: e.matmul(banks[3 + ps][:, qb0 * 128:(qb0 + nqb) * 128], vA[:, kb, :], Et[:, bnk, c0:c0 + nqb * 128],
                                                          start=False, stop=last),
                                 reads=["vA", ("Eb", sl, bnk)], writes=[("bank", 3 + ps)])
                            P.op("pe", lambda e: e.matmul(banks[5 + ps][0:1, qb0 * 128:(qb0 + nqb) * 128], ones_bf[:, 0:1], Et[:, bnk, c0:c0 + nqb * 128],
                                                          start=False, stop=last),
                                 reads=["ones_bf", ("Eb", sl, bnk)], writes=[("bank", 5 + ps)])
                    emit_epi(nqt - 1, (tl - 1) % 2)
        P.barrier()

    if "p3" in phases:
        P.arena_reset()
        KT = [P.asb("KT", [128, 2, 8192], BF16) for _ in range(2)]
        VB = [P.asb("VB", [128, 64, 128], BF16) for _ in range(2)]
        QPs = [P.asb("QP", [128, 2, 4096], BF16) for _ in range(2)]
        QMs = [P.asb("QM", [128, 2, 4096], BF16) for _ in range(2)]
        Et = [P.asb("E", [128, 2, 512], BF16) for _ in range(3)]
        accz = P.asb("accz", [128, 2, 512], F32)
        O1 = P.asb("O1", [128, 512], F32)
        O2 = P.asb("O2", [128, 512], F32)
        rz = P.asb("rz3", [64, 512], F32)
        zh3 = P.asb("zh3", [64, 512], BF16)
        zl3 = P.asb("zl3", [64, 512], BF16)
        ob = P.asb("ob", [128, 512], F32)
        sq = P.asb("sq", [128, 512], F32)
        stg = [P.asb("stgB", [128, 512], BF16) for _ in range(2)]
        heads = [(ji, h) for ji in range(len(JOBS)) for h in range(8)]

        def emit_loads(idx):
            ji_, h_ = heads[idx]
            job_ = JOBS[ji_]
            tok0_, n_own_, n_tot_, own0_ = job_["tok0"], job_["n_own"], job_["n_tot"], job_["own0"]
            b_ = idx % 2
            for m in range(2):
                P.dma("sp", KT[b_][0:64, m, 0:n_tot_], kbT[h_, m * 64:(m + 1) * 64, tok0_:tok0_ + n_tot_], writes=[("KT", b_, m)])
                P.dma("pool", KT[b_][64:72, m, 0:n_tot_], kaug8[h_, :, tok0_:tok0_ + n_tot_], writes=[("KTa", b_, m)])
            P.dma("sp", VB[b_][:, 0:n_tot_ // 128, :], vb[h_, tok0_:tok0_ + n_tot_, :].rearrange("(b p) d -> p b d", p=128), writes=[("VB", b_)])
            for m in range(2):
                P.dma("sp", QPs[b_][0:64, m, 0:n_own_], qbT[h_, m * 64:(m + 1) * 64, own0_:own0_ + n_own_], writes=[("QP", b_, m)])
                P.dma("sp", QMs[b_][0:64, m, 0:n_own_], qbT[h_, m * 64:(m + 1) * 64, own0_:own0_ + n_own_], writes=[("QM", b_, m)])
                P.dma("pool", QPs[b_][64:72, m, 0:n_own_], qaugP[:, own0_:own0_ + n_own_], writes=[("QPa", b_, m)])
                P.dma("pool", QMs[b_][64:72, m, 0:n_own_], qaugM[:, own0_:own0_ + n_own_], writes=[("QMa", b_, m)])

        pending = []
        stc = 0
        emit_loads(0)
        for hidx, (ji, h) in enumerate(heads):
            job = JOBS[ji]
            tok0, n_own, n_tot, own0 = job["tok0"], job["n_own"], job["n_tot"], job["own0"]
            nqt = n_own // 512
            nb = n_tot // 128
            nbo = n_own // 128
            if True:
                kb_ = hidx % 2
                if hidx + 1 < len(heads):
                    emit_loads(hidx + 1)
                KTt = KT[kb_]
                VBt = VB[kb_]
                QP = QPs[kb_]
                QM = QMs[kb_]
                def keep(qt, kb):
                    if ALIBI_SKIP_T is None:
                        return True
                    if kb < nbo:
                        if qt * 4 <= kb < qt * 4 + 4:
                            return True
                        dmin = qt * 512 - (kb * 128 + 127) if kb < qt * 4 else kb * 128 - (qt * 512 + 511)
                    else:
                        i = kb - nbo
                        slot_d = (max(0, i - 1) if ji == 0 else i // 2) * 128
                        dmin = min(qt, nqt - 1 - qt) * 512 + slot_d + 1
                    return float(2.0 ** -(h + 1)) * dmin < ALIBI_SKIP_T

                steps = []
                for qt in range(nqt):
                    kbs = [kb for kb in range(nb) if keep(qt, kb)]
                    for pi, kb in enumerate(kbs):
                        steps.append((qt, kb, pi, len(kbs)))

                def emit_QK(si):
                    qt, kb, _pi, _nk = steps[si]
                    slot = si % 2
                    q0 = qt * 512
                    kc = slice(kb * 128, (kb + 1) * 128)
                    for m in range(2):
                        pS = pairs[slot][:, m, :]
                        bk = ("pS", slot, m)
                        rk = [("KT", kb_, m), ("KTa", kb_, m)]
                        rp = rk + [("QP", kb_, m), ("QPa", kb_, m)]
                        rm = rk + [("QM", kb_, m), ("QMa", kb_, m)]
                        if kb >= nbo or kb < qt * 4:
                            P.op("pe", lambda e: e.matmul(pS, KTt[0:72, m, kc], QP[0:72, m, q0:q0 + 512], start=True, stop=True),
                                 reads=rp, writes=[bk])
                        elif kb >= qt * 4 + 4:
                            P.op("pe", lambda e: e.matmul(pS, KTt[0:72, m, kc], QM[0:72, m, q0:q0 + 512], start=True, stop=True),
                                 reads=rm, writes=[bk])
                        else:
                            d = kb - qt * 4
                            if d > 0:
                                P.op("pe", lambda e: e.matmul(pS[:, 0:d * 128], KTt[0:72, m, kc], QM[0:72, m, q0:q0 + d * 128],
                                                              start=True, stop=True), reads=rm, writes=[bk])
                            P.op("pe", lambda e: e.matmul(pS[:, d * 128:(d + 1) * 128], KTt[0:64, m, kc],
                                                          QP[0:64, m, q0 + d * 128:q0 + (d + 1) * 128], start=True, stop=True),
                                 reads=rp, writes=[bk])
                            if d < 3:
                                P.op("pe", lambda e: e.matmul(pS[:, (d + 1) * 128:512], KTt[0:72, m, kc],
                                                              QP[0:72, m, q0 + (d + 1) * 128:q0 + 512], start=True, stop=True),
                                     reads=rp, writes=[bk])
                            P.op("dve", lambda e: e.tensor_tensor(pS[:, d * 128:(d + 1) * 128], pS[:, d * 128:(d + 1) * 128],
                                                                  dtabB[:, h, :], ALU.add),
                                 reads=[bk, "dtabB"], writes=[bk])

                emit_QK(0)
                if len(steps) > 1:
                    emit_QK(1)
                for si, (qt, kb, pi, nk) in enumerate(steps):
                    slot = si % 2
                    es = si % 3
                    P.op("act", lambda e: e.activation(Et[es][:], pairs[slot][:], AF.Exp),
                         reads=[("pS", slot, 0), ("pS", slot, 1)], writes=[("E", es)])
                    if si + 2 < len(steps):
                        emit_QK(si + 2)
                    for m in range(2):
                        P.op("pe", lambda e: e.matmul(banks[4 + m], VBt[:, kb, :], Et[es][:, m, :], start=(pi == 0), stop=(pi == nk - 1)),
                             reads=[("VB", kb_), ("E", es)], writes=[("bank", 4 + m)])
                    last_pe = (nk - 1) // 3 * 3
                    if pi % 3 == 0:
                        P.op("pe", lambda e: e.matmul(banks[6][0:32, :], ones32, Et[es][:, 0, :], start=(pi == 0), stop=(pi == last_pe),
                                                      tile_position=(0, 0)),
                             reads=["ones32", ("E", es)], writes=[("bank", 6)])
                        P.op("pe", lambda e: e.matmul(banks[6][32:64, :], ones32, Et[es][:, 1, :], start=(pi == 0), stop=(pi == last_pe),
                                                      tile_position=(0, 32)),
                             reads=["ones32", ("E", es)], writes=[("bank", 6)])
                    elif pi == 1:
                        P.op("dve", lambda e: e.tensor_copy(accz[:], Et[es][:]), reads=[("E", es)], writes=["accz"])
                    else:
                        P.op("dve", lambda e: e.tensor_tensor(accz[:], accz[:], Et[es][:], ALU.add), reads=[("E", es), "accz"], writes=["accz"])
                    if pi == nk - 1:
                        P.op("pe", lambda e: e.matmul(banks[7][0:64, :], self32[:, 0:64], accz[:, 0, :], start=True, stop=False),
                             reads=["self32", "accz"], writes=[("bank", 7)])
                        P.op("pe", lambda e: e.matmul(banks[7][0:64, :], self32[:, 64:128], accz[:, 1, :], start=False, stop=True),
                             reads=["self32", "accz"], writes=[("bank", 7)])
                        P.op("dve", lambda e: e.tensor_copy(O1[:], banks[4]), reads=[("bank", 4)], writes=["O1"])
                        P.op("dve", lambda e: e.tensor_copy(O2[:], banks[5]), reads=[("bank", 5)], writes=["O2"])
                        P.op("dve", lambda e: e.tensor_copy(rz[0:33, :], banks[7][0:33, :]), reads=[("bank", 7)], writes=["rz3"])
                        P.op("dve", lambda e: e.tensor_tensor(rz[0:33, :], rz[0:33, :], banks[6][0:33, :], ALU.add), reads=[("bank", 6), "rz3"], writes=["rz3"])
                        P.op("dve", lambda e: e.tensor_copy(zh3[0:33, :], rz[0:33, :]), reads=["rz3"], writes=["zh3"])
                        P.op("dve", lambda e: e.tensor_tensor(zl3[0:33, :], rz[0:33, :], zh3[0:33, :], ALU.subtract), reads=["rz3", "zh3"], writes=["zl3"])
                        st = stc % 2
                        stc += 1

                        def stage2():
                            P.op("pe", lambda e: e.matmul(banks[7], ones_bw[0:1, :], zh3[0:1, :], start=True, stop=False),
                                 reads=["ones32", "zh3"], writes=[("bank", 7)])
                            P.op("pe", lambda e: e.matmul(banks[7], ones_bw[0:1, :], zl3[0:1, :], start=False, stop=True),
                                 reads=["ones32", "zl3"], writes=[("bank", 7)])
                            P.op("act", lambda e: e.activation(ob[:], banks[7], AF.Ln), reads=[("bank", 7)], writes=["ob"])
                            P.op("act", lambda e: e.activation(ob[:], ob[:], AF.Exp, scale=-1.0), reads=["ob"], writes=["ob"])

                        def stage3():
                            P.op("dve", lambda e: e.tensor_tensor(O1[:], O1[:], ob[:], ALU.mult), reads=["O1", "ob"], writes=["O1"])
                            P.op("pe", lambda e: e.matmul(banks[7], ones_bw[32:33, :], zh3[32:33, :], start=True, stop=False),
                                 reads=["ones32", "zh3"], writes=[("bank", 7)])
                            P.op("pe", lambda e: e.matmul(banks[7], ones_bw[32:33, :], zl3[32:33, :], start=False, stop=True),
                                 reads=["ones32", "zl3"], writes=[("bank", 7)])
                            P.op("act", lambda e: e.activation(sq[:], banks[7], AF.Ln), reads=[("bank", 7)], writes=["sq"])
                            P.op("act", lambda e: e.activation(sq[:], sq[:], AF.Exp, scale=-1.0), reads=["sq"], writes=["sq"])

                        def stage4():
                            P.op("dve", lambda e: e.tensor_tensor(O2[:], O2[:], sq[:], ALU.mult), reads=["O2", "sq"], writes=["O2"])
                            P.op("dve", lambda e: e.scalar_tensor_tensor(out=ob[:], in0=O2[:], scalar=neglam[:, 0:1], in1=O1[:],
                                                                         op0=ALU.mult, op1=ALU.add),
                                 reads=["O1", "O2", "neglam"], writes=["ob"])
                            P.op("pool", lambda e: e.tensor_tensor(sq[:], ob[:], ob[:], ALU.mult), reads=["ob"], writes=["sq"])

                        def stage5(h=h, qt=qt, own0=own0, st=st):
                            P.op("pe", lambda e: e.matmul(banks[7], ones_f[:], sq[:], start=True, stop=True),
                                 reads=["ones_f", "sq"], writes=[("bank", 7)])
                            P.op("act", lambda e: e.activation(sq[:], banks[7], AF.Ln, scale=1.0 / 128, bias=epsb[:]),
                                 reads=[("bank", 7), "epsb"], writes=["sq"])
                            P.op("act", lambda e: e.activation(sq[:], sq[:], AF.Exp, scale=-0.5), reads=["sq"], writes=["sq"])

                        def stage6(h=h, qt=qt, own0=own0, st=st):
                            P.op("dve", lambda e: e.scalar_tensor_tensor(out=stg[st][:], in0=ob[:], scalar=gsub[:, 0:1], in1=sq[:],
                                                                         op0=ALU.mult, op1=ALU.mult),
                                 reads=["ob", "sq", "gsub"], writes=[("stgB", st)])
                            P.dma("pool", attT[8 + h, :, own0 + qt * 512:own0 + (qt + 1) * 512], stg[st][:], reads=[("stgB", st)])

                        pending.extend([stage2, stage3, stage4, stage5, stage6])
                    elif pending:
                        pending.pop(0)()
        while pending:
            pending.pop(0)()
        P.barrier()

    while late_casts:
        emit_late_cast()
    if "p4" in phases:
        P.arena_reset()
        gtm = P.asb("gtab_mlp", [128, D], F32)
        gtf = P.asb("gtab_fin", [128, D], F32)
        x1 = P.asb("x1", [128, 4, D], F32)
        aT = P.asb("aT", [128, NCH, 512], BF16)
        h2T_off = (P.arena_ptr + 63) // 64 * 64
        h2T = P.asb("h2T", [128, NCH, 512], BF16)
        yo = [nc.alloc_sbuf_tensor_at("yo%d" % i, [128, D], F32, offset=h2T_off + i * D * 4) for i in range(2)]
        uT = P.asb("uT", [128, 64, 512], BF16)
        wg = [P.asb("wg4", [128, NCH, 512], BF16) for _ in range(2)]
        hb = [P.asb("hb4", [128, D], BF16) for _ in range(1)]
        junk = P.asb("junk4", [128, D], BF16)
        rl_off = (P.arena_ptr + 63) // 64 * 64
        rl = [P.asb("rl", [128, 512], F32) for _ in range(2)]
        hb.append(nc.alloc_sbuf_tensor_at("hb4_alias", [128, D], BF16, offset=rl_off))
        P.dma("sp", gtm[:], g_mlp[0].partition_broadcast(128), writes=["gtm"])
        P.dma("sp", gtf[:], g_fin[0].partition_broadcast(128), writes=["gtf"])
        wgi = RR(range(2))
        pgi = RR([2, 3, 4, 5, 6, 7])
        cnt4 = 0
        def load_x1(ti, j):
            tok = OWN_TILES[ti] * 512 + j * 128
            P.dma("pool", x1[:, j, :], xc[tok:tok + 128, :], writes=[("x1", j, n) for n in range(4)])

        def load_aT(ti):
            P.dma("sp", aT[:], attT[:, :, ti * 512:(ti + 1) * 512].rearrange("c p t -> p c t"), writes=["aT"])

        load_aT(0)
        for j in range(4):
            load_x1(0, j)
        for ti, tt in enumerate(OWN_TILES):
            o0 = ti * 512
            tok = tt * 512
            for nt in range(4):
                ws = wgi.next()
                wt = wg[ws]
                P.dma("sp", wt[:], w_out_bf[nt],
                      reads=[("w_out_bf", nt)], writes=[("wg", ws)])
                for j in range(4):
                    b = pgi.next()
                    for kc in range(NCH):
                        P.op("pe", lambda e, b=b, kc=kc, j=j, wt=wt: e.matmul(
                            banks[b], aT[:, kc, j * 128:(j + 1) * 128], wt[:, kc, :], start=(kc == 0), stop=(kc == NCH - 1)),
                            reads=["aT", ("wg", ws)], writes=[("bank", b)])
                    P.op("dve", lambda e, b=b, j=j, nt=nt: e.tensor_tensor(
                        x1[:, j, nt * 512:(nt + 1) * 512], banks[b], x1[:, j, nt * 512:(nt + 1) * 512], ALU.add),
                        reads=[("bank", b), ("x1", j, nt)], writes=[("x1", j, nt)])
            if ti + 1 < len(OWN_TILES):
                load_aT(ti + 1)
            hbk = [("hb", 0), ("rl", 0)]
            for j in range(5):
                if j < 4:
                    cnt4 += 1
                    hk = [hbk[j % 2]] + ([("rl", 1)] if j % 2 == 1 else [])
                    norm_front(x1[:, j, :], [("x1", j, n) for n in range(4)], gtm, "gtm", hb[j % 2], hk[0], cnt4 % 4, junk)
                    if j % 2 == 1:
                        P.last_writer[("rl", 1)] = P.last_writer[("rl", 0)]
                        P.readers[("rl", 1)] = ({}, [])
                if j >= 1:
                    norm_back(hb[(j - 1) % 2], hbk[(j - 1) % 2], h2T, lambda jj, half: ("h2T", jj, half), j - 1)
            for fg in range(16):
                ws = wgi.next()
                wt = wg[ws]
                P.dma("sp", wt[:], w_up_bf[fg],
                      reads=[("w_up_bf", fg)], writes=[("wg", ws)])
                for ci in range(4):
                    ffc = fg * 4 + ci
                    b = pgi.next()
                    for kc in range(NCH):
                        P.op("pe", lambda e, b=b, kc=kc, ci=ci, wt=wt: e.matmul(
                            banks[b], wt[:, kc, ci * 128:(ci + 1) * 128], h2T[:, kc, :], start=(kc == 0), stop=(kc == NCH - 1)),
                            reads=[("wg", ws)] + [("h2T", jj, kc // 8) for jj in range(4)], writes=[("bank", b)])
                    rs = ffc % 2
                    P.op("act", lambda e, b=b, rs=rs: e.activation(rl[rs][:], banks[b], AF.Relu), reads=[("bank", b)], writes=[("rl", rs)])
                    P.op("pool", lambda e, rs=rs, ffc=ffc: e.tensor_tensor(uT[:, ffc, :], rl[rs][:], rl[rs][:], ALU.mult),
                         reads=[("rl", rs)], writes=[("uT", ffc)])
            for nt in range(4):
                for fg in range(4):
                    ws = wgi.next()
                    wt = wg[ws]
                    P.dma("sp", wt[:], w_down_bf[nt * 4 + fg],
                          reads=[("w_down_bf", nt, fg)], writes=[("wg", ws)])
                    for j in range(4):
                        for fc in range(16):
                            ffc = fg * 16 + fc
                            P.op("pe", lambda e, j=j, fc=fc, ffc=ffc, wt=wt, fg=fg: e.matmul(
                                banks[4 + j], uT[:, ffc, j * 128:(j + 1) * 128], wt[:, fc, :],
                                start=(fg == 0 and fc == 0), stop=(fg == 3 and fc == 15)),
                                reads=[("uT", ffc), ("wg", ws)], writes=[("bank", 4 + j)])
                for j in range(4):
                    P.op("dve", lambda e, j=j, nt=nt: e.tensor_tensor(
                        x1[:, j, nt * 512:(nt + 1) * 512], banks[4 + j], x1[:, j, nt * 512:(nt + 1) * 512], ALU.add),
                        reads=[("bank", 4 + j), ("x1", j, nt)], writes=[("x1", j, nt)])
            for j in range(4):
                cnt4 += 1
                si = cnt4 % 4
                ys = cnt4 % 2
                xk = [("x1", j, n) for n in range(4)]
                P.op("act", lambda e, j=j, si=si: e.activation(junk[:], x1[:, j, :], AF.Square, accum_out=ss[:, si:si + 1]),
                     reads=xk, writes=["junk", ("ss", si)])
                P.op("act", lambda e, si=si: e.activation(rstd[:, si:si + 1], ss[:, si:si + 1], AF.Ln, scale=1.0 / D, bias=epsb[:]),
                     reads=[("ss", si), "epsb"], writes=[("rstd", si)])
                P.op("act", lambda e, si=si: e.activation(rstd[:, si:si + 1], rstd[:, si:si + 1], AF.Exp, scale=-0.5),
                     reads=[("rstd", si)], writes=[("rstd", si)])
                yk = [("h2T", jj, ys) for jj in range(4)]
                P.op("dve", lambda e, j=j, si=si, ys=ys: e.scalar_tensor_tensor(out=yo[ys][:], in0=x1[:, j, :], scalar=rstd[:, si:si + 1],
                                                                                in1=gtf[:], op0=ALU.mult, op1=ALU.mult),
                     reads=xk + [("rstd", si), "gtf"], writes=yk)
                P.dma("pool", y[o0 + j * 128:o0 + (j + 1) * 128, :], yo[ys][:], reads=yk)
                if ti + 1 < len(OWN_TILES):
                    load_x1(ti + 1, j)

    P.emit()
    ncd.__exit__(None, None, None)
    lp.__exit__(None, None, None)
    return nc, P


def _slopes():
    i = np.arange(1, 17, dtype=np.float32)
    s = np.exp2(np.float32(-8.0 / 16.0) * i).astype(np.float32)
    return s[0::2], s[1::2]


def _order_others(after, before):
    a = [after[i * 128:(i + 1) * 128] for i in range(len(after) // 128)]
    b = [before[i * 128:(i + 1) * 128] for i in range(len(before) // 128)][::-1]
    out = []
    if a and b:
        while a or b:
            if a:
                out.append(a.pop(0))
            if b:
                out.append(b.pop(0))
    elif a:
        out = a
    else:
        out = [b[1], b[0]] + b[2:]
    return np.concatenate(out)


def _core_layout(c):
    hf = c % 2
    so = np.arange(hf * 4096, hf * 4096 + 4096)
    sa = np.arange((hf + 1) * 4096, 8192)
    sbf = np.arange(0, hf * 4096)
    qt = c % 4
    po = np.arange(qt * 1024, qt * 1024 + 1024)
    pa = np.arange(qt * 1024 + 1024, 4096)
    pb = np.arange(0, qt * 1024)
    return ((c // 2, np.concatenate([so, _order_others(sa, sbf)]), hf * 4096, 0),
            (c // 4, np.concatenate([po, _order_others(pa, pb)]), qt * 1024, 0))


def _pos_tables(c):
    sl_a, sl_b = _slopes()
    (s, sord, sstart, s_na), (p, pord, pstart, p_na) = _core_layout(c)
    kaug8 = np.zeros((8, 8, NTOK), np.float32)
    qaugP = np.zeros((8, NOWN), np.float32)
    qaugM = np.zeros((8, NOWN), np.float32)
    for (order, start, n_own, n_after, tok0, own0) in ((sord, sstart, 4096, s_na, 0, 0), (pord, pstart, 1024, p_na, 8192, 4096)):
        rel = (order - start).astype(np.int64)
        hi = np.floor_divide(rel, 128)
        lo = rel - 128 * hi
        n = len(order)
        sig = np.where(rel < 0, 1.0, -1.0).astype(np.float32)
        sig[:n_own] = 1.0
        for h in range(8):
            sl = sl_b[h]
            plus = np.stack([sig * sl * 128.0 * hi, sig * sl * lo, -sig * sl * np.ones(n), -sig * sl * np.ones(n)]).astype(np.float32)
            kaug8[h, 0:4, tok0:tok0 + n] = plus
            kaug8[h, 4:8, tok0:tok0 + n_own] = -plus[:, 0:n_own]
        qa = np.stack([np.ones(n_own), np.ones(n_own), 128.0 * hi[:n_own], lo[:n_own]]).astype(np.float32)
        qaugP[0:4, own0:own0 + n_own] = qa
        qaugM[4:8, own0:own0 + n_own] = qa
    hf, qt = c % 2, c % 4
    hm = np.zeros((128, 4), np.float32)
    hm[:, 0] = 0.0 if hf == 1 else NEG
    hm[:, 1] = 0.0 if hf == 0 else NEG
    hm[:, 2] = 0.0 if qt > 0 else NEG
    hm[:, 3] = 0.0 if qt < 3 else NEG
    return kaug8, qaugP, qaugM, hm


def _static_tables():
    sl_a, sl_b = _slopes()
    k = np.arange(128)[:, None]
    q = np.arange(128)[None, :]
    dtabB = np.stack([-sl_b[h] * np.abs(q - k).astype(np.float32) for h in range(8)], axis=1).astype(np.float32)
    tabA = np.zeros((128, 8, 3, 128), np.float32)
    dists = [128 + k - q, np.abs(q - k), 128 + q - k]
    for h in range(8):
        for r in range(3):
            dd = dists[r].astype(np.float32)
            tabA[:, h, r, :] = np.where(dd <= 128, -sl_a[h] * dd, NEG)
    return dtabB.reshape(128, -1), tabA.reshape(128, -1)


_CACHE = {}


def kernel(x_prompt, x_sample, norm_attn_g, w_in, sink_logits, lambda_q1, lambda_k1, lambda_q2, lambda_k2,
           diff_subln_g, w_out, norm_mlp_g, w_up, w_down, norm_final_g):
    f = lambda a: np.ascontiguousarray(np.asarray(a, dtype=np.float32))
    x_prompt, x_sample = f(x_prompt), f(x_sample)
    if "nc" not in _CACHE:
        _CACHE["nc"] = build()[0]
    nc = _CACHE["nc"]
    dtabB, tabA = _static_tables()
    shared = dict(w_in=f(w_in)[0], w_out=f(w_out)[0], w_up=f(w_up)[0], w_down=f(w_down)[0],
                  g_attn=f(norm_attn_g).reshape(1, D), g_mlp=f(norm_mlp_g).reshape(1, D), g_fin=f(norm_final_g).reshape(1, D),
                  sink=f(sink_logits).reshape(1, 8), lq1=f(lambda_q1).reshape(1, 64), lk1=f(lambda_k1).reshape(1, 64),
                  lq2=f(lambda_q2).reshape(1, 64), lk2=f(lambda_k2).reshape(1, 64), subg=f(diff_subln_g).reshape(1, 128),
                  dtabB=dtabB, tabA=tabA)
    in_maps = []
    lay = []
    for c in range(8):
        (s, sord, sstart, _), (p, pord, pstart, _) = _core_layout(c)
        kaug8, qaugP, qaugM, hm = _pos_tables(c)
        xcore = np.concatenate([x_sample[s][sord], x_prompt[p][pord]], axis=0)
        m = dict(shared)
        m.update(xc=np.ascontiguousarray(xcore), kaug8=kaug8, qaugP=qaugP, qaugM=qaugM, hmask=hm)
        in_maps.append(m)
        lay.append((s, sstart, p, pstart))
    res = run_bass_kernel_spmd(nc, in_maps, core_ids=list(range(8)))
    y_prompt = np.zeros((2, 4096, D), np.float32)
    y_sample = np.zeros((4, 8192, D), np.float32)
    for c in range(8):
        yc = np.asarray(res.results[c]["y"])
        s, sstart, p, pstart = lay[c]
        y_sample[s, sstart:sstart + 4096] = yc[0:4096]
        y_prompt[p, pstart:pstart + 1024] = yc[4096:5120]
    return (y_prompt, y_sample)
```

```python
import math
import numpy as np
import concourse.bass as bass
import concourse.mybir as mybir
from concourse.bass_utils import run_bass_kernel_spmd

F32 = mybir.dt.float32
BF16 = mybir.dt.bfloat16
AF = mybir.ActivationFunctionType
ALU = mybir.AluOpType

D = 2048
NCH = 16
INW = 4608
DFF = 8192
NTOK = 12288
NOWN = 5120
NEG = -30000.0
EPS = 1e-5
LAMBDA_INIT = 0.8 - 0.6 * math.exp(0.0)
ALIBI_SKIP_T = 176.0
JOBS = [dict(tok0=0, n_own=4096, n_tot=8192, own0=0), dict(tok0=8192, n_own=1024, n_tot=4096, own0=4096)]
OWN_TILES = list(range(8)) + [16, 17]


class _Op:
    __slots__ = ("eng", "fn", "deps", "is_dma", "sig", "sem", "val", "ring_wait", "bar")

    def __init__(self, eng, fn, is_dma):
        self.eng = eng
        self.fn = fn
        self.deps = []
        self.is_dma = is_dma
        self.sig = False
        self.sem = None
        self.val = 0
        self.ring_wait = None
        self.bar = None


class _Rec:
    def __getattr__(self, name):
        def f(*a, **k):
            self.call = (name, a, k)
        return f


class Prog:
    ENGS = ("pe", "act", "dve", "pool", "sp")
    DMAQ = ("sp", "pool", "act")
    RING = 8

    def __init__(self, nc):
        self.nc = nc
        self.ops = []
        self.last_writer = {}
        self.readers = {}
        self.last_op = {}
        self.arena_base = None

    def arena_init(self):
        self.arena_base = self.nc.sbuf_base
        self.arena_top = self.nc.sbuf_top
        self.arena_ptr = self.arena_base
        self._an = 0

    def arena_reset(self):
        self.arena_ptr = self.arena_base

    def asb(self, name, shape, dtype):
        nbytes = int(np.prod(shape[1:])) * (2 if dtype == BF16 else 4)
        off = (self.arena_ptr + 63) // 64 * 64
        assert off + nbytes <= self.arena_top, ("SBUF arena overflow", name, off + nbytes - self.arena_top)
        self.arena_ptr = off + nbytes
        self._an += 1
        return self.nc.alloc_sbuf_tensor_at("%s_%d" % (name, self._an), list(shape), dtype, offset=off)

    def sb(self, name, shape, dtype):
        return self.nc.alloc_sbuf_tensor(name, list(shape), dtype)

    def _record(self, op, reads, writes):
        deps = {}
        for r in reads:
            w = self.last_writer.get(r)
            if w is not None:
                deps[id(w)] = (w, True)
        for wk in writes:
            w = self.last_writer.get(wk)
            if w is not None and id(w) not in deps:
                deps[id(w)] = (w, False)
            rd = self.readers.get(wk)
            if rd:
                for o in rd[0].values():
                    if id(o) not in deps:
                        deps[id(o)] = (o, False)
                for o in rd[1]:
                    if id(o) not in deps:
                        deps[id(o)] = (o, False)
        for d, raw in deps.values():
            if d is op:
                continue
            if d.eng == op.eng and not d.is_dma and not op.is_dma:
                if op.eng == "pe" or not raw:
                    continue
            op.deps.append(d)
            d.sig = True
        for r in reads:
            rd = self.readers.get(r)
            if rd is None:
                rd = self.readers[r] = ({}, [])
            if op.is_dma:
                rd[1].append(op)
            else:
                rd[0][op.eng] = op
        for wk in writes:
            self.last_writer[wk] = op
            self.readers[wk] = ({}, [])
        if not op.is_dma:
            self.last_op[op.eng] = op
        self.ops.append(op)
        return op

    def op(self, eng, fn, reads=(), writes=()):
        r = _Rec()
        fn(r)
        name, a, k = r.call
        return self._record(_Op(eng, lambda e: getattr(e, name)(*a, **k), False), reads, writes)

    def i(self, eng, name, *args, reads=(), writes=(), **kw):
        return self._record(_Op(eng, lambda e: getattr(e, name)(*args, **kw), False), reads, writes)

    def dma(self, eng, out, in_, reads=(), writes=()):
        o = _Op(eng, lambda e: e.dma_start(out=out, in_=in_), True)
        o.sig = True
        return self._record(o, reads, writes)

    def barrier(self):
        b = _Op(None, None, False)
        b.bar = {}
        for e, o in self.last_op.items():
            o.sig = True
        b.deps = list(self.last_op.values())
        self.ops.append(b)
        self.last_writer = {}
        self.readers = {}
        self.last_op = {}

    def emit(self):
        nc = self.nc
        sems = {e: nc.alloc_semaphore("s_" + e) for e in self.ENGS}
        rings = {e: [nc.alloc_semaphore("r_%s%d" % (e, i)) for i in range(self.RING)] for e in self.DMAQ}
        cnt = {e: 0 for e in self.ENGS}
        dcnt = {e: 0 for e in rings}
        ringval = {}
        per_eng = {e: [] for e in self.ENGS}
        for o in self.ops:
            if o.bar is not None:
                o.bar = dict(ringval)
                for e in self.ENGS:
                    per_eng[e].append(o)
                continue
            if o.is_dma:
                n = dcnt[o.eng]
                dcnt[o.eng] += 1
                o.sem = rings[o.eng][n % self.RING]
                o.val = 16 * (n // self.RING + 1)
                ringval[id(o.sem)] = (o.sem, o.val)
                if n >= self.RING:
                    o.ring_wait = (o.sem, 16 * (n // self.RING))
            elif o.sig:
                cnt[o.eng] += 1
                o.sem = sems[o.eng]
                o.val = cnt[o.eng]
            per_eng[o.eng].append(o)
        self.stats = dict(cnt=dict(cnt), dcnt=dict(dcnt), nops={e: len(v) for e, v in per_eng.items()})
        final_rings = dict(ringval)

        def run(eng_name, e):
            waited = {}

            def wait(sem, val):
                k = id(sem)
                if waited.get(k, 0) >= val:
                    return
                waited[k] = val
                e.wait_ge(sem, val)

            for o in per_eng[eng_name]:
                if o.bar is not None:
                    for d in o.deps:
                        if d.eng != eng_name:
                            wait(d.sem, d.val)
                    for sem, val in o.bar.values():
                        wait(sem, val)
                    continue
                for d in o.deps:
                    wait(d.sem, d.val)
                if o.ring_wait is not None:
                    wait(*o.ring_wait)
                ins = o.fn(e)
                if o.sig:
                    ins.then_inc(o.sem, 16 if o.is_dma else 1)
            if eng_name == "sp":
                for sem, val in final_rings.values():
                    wait(sem, val)

        with nc.Block() as block:
            @block.tensor
            def _(e):
                run("pe", e)

            @block.scalar
            def _(e):
                run("act", e)

            @block.vector
            def _(e):
                run("dve", e)

            @block.gpsimd
            def _(e):
                run("pool", e)

            @block.sync
            def _(e):
                run("sp", e)


class RR:
    def __init__(self, items):
        self.items = list(items)
        self.i = 0

    def next(self):
        v = self.items[self.i % len(self.items)]
        self.i += 1
        return v


def build(phases=("cast", "p1", "p2", "p3", "p4"), dbg=False):
    nc = bass.Bass("TRN2", target_bir_lowering=False)
    P = Prog(nc)
    lp = nc.allow_low_precision("bf16 matmul operands, fp32 accumulation")
    lp.__enter__()
    ncd = nc.allow_non_contiguous_dma("tiny parameter-vector loads")
    ncd.__enter__()

    def din(name, shape):
        return nc.dram_tensor(name, list(shape), F32, kind="ExternalInput").ap()

    def dscr(name, shape, dt=BF16):
        kind = "ExternalOutput" if dbg else "Internal"
        return nc.dram_tensor(name, list(shape), dt, kind=kind).ap()

    xc = din("xc", [NTOK, D])
    w_in = din("w_in", [D, INW])
    w_out = din("w_out", [D, D])
    w_up = din("w_up", [D, DFF])
    w_down = din("w_down", [DFF, D])
    g_attn = din("g_attn", [1, D])
    g_mlp = din("g_mlp", [1, D])
    g_fin = din("g_fin", [1, D])
    sink = din("sink", [1, 8])
    lq1 = din("lq1", [1, 64])
    lk1 = din("lk1", [1, 64])
    lq2 = din("lq2", [1, 64])
    lk2 = din("lk2", [1, 64])
    subg = din("subg", [1, 128])
    qaugP = din("qaugP", [8, NOWN])
    qaugM = din("qaugM", [8, NOWN])
    kaug8 = din("kaug8", [8, 8, NTOK])
    dtabB_d = din("dtabB", [128, 8 * 128])
    tabA_d = din("tabA", [128, 8 * 3 * 128])
    hmask_d = din("hmask", [128, 4])
    y = nc.dram_tensor("y", [NOWN, D], F32, kind="ExternalOutput").ap()

    w_in_bf = dscr("w_in_bf", [9, 128, NCH, 512])
    w_out_bf = dscr("w_out_bf", [4, 128, NCH, 512])
    w_up_bf = dscr("w_up_bf", [16, 128, NCH, 512])
    w_down_bf = dscr("w_down_bf", [16, 128, NCH, 512])
    qaT = dscr("qaT", [8, 128, NOWN])
    kaT = dscr("kaT", [2, 128, NTOK])
    va = dscr("va", [2, NTOK, 128])
    qbT = dscr("qbT", [8, 128, NOWN])
    kbT = dscr("kbT", [8, 128, NTOK])
    vb = dscr("vb", [8, NTOK, 128])
    attT = dscr("attT", [16, 128, NOWN])

    ident = P.sb("ident", [128, 128], BF16)
    identf = P.sb("identf", [128, 128], F32)
    ones_f = P.sb("ones_f", [128, 128], F32)
    ones_bf = P.sb("ones_bf", [128, 2], BF16)
    ones_bw = P.sb("ones_bw", [128, 128], BF16)
    sel = [P.sb("sel0", [128, 64], BF16), P.sb("sel1", [128, 64], BF16)]
    self32 = P.sb("self32", [128, 128], F32)
    epsb = P.sb("epsb", [128, 1], F32)
    tabA = P.sb("tabA_sb", [128, 8, 3, 128], F32)
    dtabB = P.sb("dtabB_sb", [128, 8, 128], F32)
    hmask = P.sb("hmask_sb", [128, 4], F32)
    esink = P.sb("esink", [1, 8], F32)
    lamv = P.sb("lamv", [128, 4, 64], F32)
    lamw = P.sb("lamw", [128, 8], F32)
    neglam = P.sb("neglam", [128, 1], F32)
    gsub = P.sb("gsub", [128, 1], F32)
    ss = P.sb("ss", [128, 4], F32)
    rstd = P.sb("rstd", [128, 4], F32)
    pairs = [nc.alloc_psum_tensor("pair%d" % i, [128, 2, 512], F32) for i in range(4)]
    banks = [pairs[i // 2][:, i % 2, :] for i in range(8)]
    P.arena_init()

    def bank_bf(i):
        return banks[i].bitcast(BF16).rearrange("p (c t) -> p c t", c=8)

    P.op("pool", lambda e: e.memset(identf[:], 0.0), writes=["identf"])
    P.op("pool", lambda e: e.affine_select(identf[:], identf[:], pattern=[[-1, 128]], compare_op=ALU.not_equal,
                                           fill=1.0, base=0, channel_multiplier=1), reads=["identf"], writes=["identf"])
    P.op("dve", lambda e: e.tensor_copy(ident[:], identf[:]), reads=["identf"], writes=["ident"])
    P.op("pool", lambda e: e.memset(ones_f[:], 1.0), writes=["ones_f"])
    P.op("pool", lambda e: e.memset(self32[:], 0.0), writes=["self32"])
    P.op("pool", lambda e: e.memset(self32[:, 0:1], 1.0), reads=["self32"], writes=["self32"])
    P.op("pool", lambda e: e.memset(self32[:, 96:97], 1.0), reads=["self32"], writes=["self32"])
    P.op("dve", lambda e: e.tensor_copy(sel[0][:], self32[:, 0:64]), reads=["self32"], writes=["sel0"])
    P.op("dve", lambda e: e.tensor_copy(sel[1][:], self32[:, 64:128]), reads=["self32"], writes=["sel1"])
    P.op("dve", lambda e: e.tensor_copy(ones_bf[:], ones_f[:, 0:2]), reads=["ones_f"], writes=["ones_bf"])
    P.op("dve", lambda e: e.tensor_copy(ones_bw[:], ones_f[:]), reads=["ones_f"], writes=["ones32"])
    ones32 = ones_bw[:, 0:32]
    P.op("pool", lambda e: e.memset(epsb[:], EPS), writes=["epsb"])
    P.dma("sp", tabA[:].rearrange("p a b c -> p (a b c)"), tabA_d, writes=["tabA"])
    P.dma("sp", dtabB[:].rearrange("p a b -> p (a b)"), dtabB_d, writes=["dtabB"])
    P.dma("sp", hmask[:], hmask_d, writes=["hmask"])
    P.dma("sp", esink[:], sink, writes=["esink"])
    P.op("act", lambda e: e.activation(esink[:], esink[:], AF.Exp), reads=["esink"], writes=["esink"])
    for i, v in enumerate((lq1, lk1, lq2, lk2)):
        P.dma("sp", lamv[:, i, :], v[0].partition_broadcast(128), writes=[("lamv", i)])
    P.dma("sp", gsub[:], subg.rearrange("o d -> d o"), writes=["gsub"])
    P.op("dve", lambda e: e.tensor_tensor(lamv[:, 0, :], lamv[:, 0, :], lamv[:, 1, :], ALU.mult),
         reads=[("lamv", 0), ("lamv", 1)], writes=[("lamv", 0)])
    P.op("dve", lambda e: e.tensor_tensor(lamv[:, 2, :], lamv[:, 2, :], lamv[:, 3, :], ALU.mult),
         reads=[("lamv", 2), ("lamv", 3)], writes=[("lamv", 2)])
    P.op("dve", lambda e: e.reduce_sum(lamw[:, 0:1], lamv[:, 0, :], mybir.AxisListType.X), reads=[("lamv", 0)], writes=["lamw0"])
    P.op("dve", lambda e: e.reduce_sum(lamw[:, 1:2], lamv[:, 2, :], mybir.AxisListType.X), reads=[("lamv", 2)], writes=["lamw1"])
    P.op("act", lambda e: e.activation(lamw[:, 2:4], lamw[:, 0:2], AF.Exp), reads=["lamw0", "lamw1"], writes=["lamw2"])
    P.op("dve", lambda e: e.tensor_tensor(lamw[:, 4:5], lamw[:, 3:4], lamw[:, 2:3], ALU.subtract), reads=["lamw2"], writes=["lamw4"])
    P.op("dve", lambda e: e.tensor_scalar(neglam[:], lamw[:, 4:5], -LAMBDA_INIT, None, ALU.add), reads=["lamw4"], writes=["neglam"])
    P.op("dve", lambda e: e.tensor_scalar(gsub[:], gsub[:], 1.0 - LAMBDA_INIT, None, ALU.mult), reads=["gsub"], writes=["gsub"])

    evac_rr = RR(["act", "dve"])

    def evac(dst, src, reads, writes, scale=None, eng=None):
        eng = eng or evac_rr.next()
        if eng == "act":
            if scale is None:
                P.op("act", lambda e: e.activation(dst, src, AF.Copy), reads=reads, writes=writes)
            else:
                P.op("act", lambda e: e.activation(dst, src, AF.Copy, scale=float(scale)), reads=reads, writes=writes)
        else:
            if scale is None:
                P.op(eng, lambda e: e.tensor_copy(dst, src), reads=reads, writes=writes)
            else:
                P.op(eng, lambda e: e.tensor_scalar(dst, src, float(scale), None, ALU.mult), reads=reads, writes=writes)

    late_casts = []
    if "cast" in phases:
        for g in [2, 5, 6, 7, 8, 0, 1, 3, 4]:
            P.dma("pool", w_in_bf[g], w_in[:, g * 512:(g + 1) * 512].rearrange("(kc p) n -> p kc n", p=128), writes=[("w_in_bf", g)])
        for g in range(4):
            late_casts.append((w_out_bf[g], w_out[:, g * 512:(g + 1) * 512].rearrange("(kc p) n -> p kc n", p=128), ("w_out_bf", g)))
        for g in range(16):
            late_casts.append((w_up_bf[g], w_up[:, g * 512:(g + 1) * 512].rearrange("(kc p) n -> p kc n", p=128), ("w_up_bf", g)))
        for g in range(4):
            for hh in range(4):
                late_casts.append((w_down_bf[g * 4 + hh],
                                   w_down[hh * 2048:(hh + 1) * 2048, g * 512:(g + 1) * 512].rearrange("(fc p) n -> p fc n", p=128),
                                   ("w_down_bf", g, hh)))

    def emit_late_cast():
        if late_casts:
            dst, src, key = late_casts.pop(0)
            P.dma("pool", dst, src, writes=[key])

    def norm_front(xsrc, xkeys, gtab, gkey, hb, hbkey, ssi, junk):
        P.op("act", lambda e: e.activation(junk[:], xsrc, AF.Square, accum_out=ss[:, ssi:ssi + 1]),
             reads=xkeys, writes=["junk", ("ss", ssi)])
        P.op("act", lambda e: e.activation(rstd[:, ssi:ssi + 1], ss[:, ssi:ssi + 1], AF.Ln, scale=1.0 / D, bias=epsb[:]),
             reads=[("ss", ssi), "epsb"], writes=[("rstd", ssi)])
        P.op("act", lambda e: e.activation(rstd[:, ssi:ssi + 1], rstd[:, ssi:ssi + 1], AF.Exp, scale=-0.5),
             reads=[("rstd", ssi)], writes=[("rstd", ssi)])
        P.op("dve", lambda e: e.scalar_tensor_tensor(out=hb[:], in0=xsrc, scalar=rstd[:, ssi:ssi + 1], in1=gtab[:],
                                                     op0=ALU.mult, op1=ALU.mult),
             reads=xkeys + [("rstd", ssi), gkey], writes=[hbkey])

    def norm_back(hb, hbkey, hTt, hTkeyf, j):
        for half in range(2):
            pt = bank_bf(half)
            for i in range(8):
                c = half * 8 + i
                P.op("pe", lambda e: e.transpose(pt[:, i, :], hb[:, c * 128:(c + 1) * 128], ident[:]),
                     reads=[hbkey, "ident"], writes=[("bank", half)])
            evac(hTt[:, half * 8:(half + 1) * 8, j * 128:(j + 1) * 128], pt, reads=[("bank", half)],
                 writes=[hTkeyf(j, half)])

    def norm_transpose(xsrc, xkeys, gtab, gkey, hb, hbkey, hTt, hTkeyf, j, ssi, junk):
        norm_front(xsrc, xkeys, gtab, gkey, hb, hbkey, ssi, junk)
        norm_back(hb, hbkey, hTt, hTkeyf, j)

    if "p1" in phases:
        P.arena_reset()
        gtab = P.asb("gtab_attn", [128, D], F32)
        xt = [P.asb("xt", [128, D], F32) for _ in range(3)]
        junk = P.asb("junk", [128, D], BF16)
        hb = [P.asb("hb", [128, D], BF16) for _ in range(2)]
        hT = [P.asb("hT", [128, NCH, 512], BF16) for _ in range(2)]
        wg = [P.asb("wg", [128, NCH, 512], BF16) for _ in range(3)]
        stage = [P.asb("stage", [128, 512], BF16) for _ in range(6)]
        P.dma("sp", gtab[:], g_attn[0].partition_broadcast(128), writes=["gtab"])
        pgi = RR([2, 3, 4, 5, 6, 7])
        sti = RR(range(6))
        sA = 1.0 / math.sqrt(128.0)
        NT1 = NTOK // 512
        subc = [0]

        def pro_front(tt, j):
            sub = tt * 4 + j
            s3 = sub % 3
            tok = tt * 512 + j * 128
            P.dma("sp", xt[s3][:], xc[tok:tok + 128, :], writes=[("xt", s3)])
            norm_front(xt[s3][:], [("xt", s3)], gtab, "gtab", hb[sub % 2], ("hb", sub % 2), sub % 4, junk)

        def pro_back(tt, j):
            sub = tt * 4 + j
            hs = tt % 2
            norm_back(hb[sub % 2], ("hb", sub % 2), hT[hs], lambda j, half: ("hT", hs, j, half), j)

        items = []
        for tt in range(NT1):
            own = tt in OWN_TILES
            for gi, g in enumerate([2, 5, 6, 7, 8, 0, 1, 3, 4] if own else [2, 5, 6, 7, 8]):
                items.append((tt, gi, g))

        def load_w(ii):
            tt, gi, g = items[ii]
            ws = ii % 3
            P.dma("sp", wg[ws][:], w_in_bf[g], reads=[("w_in_bf", g)], writes=[("wg", ws)])

        load_w(0)
        load_w(1)
        for j in range(4):
            pro_front(0, j)
            pro_back(0, j)
        for ii, (tt, gi, g) in enumerate(items):
            if ii + 2 < len(items):
                load_w(ii + 2)
            own = tt in OWN_TILES
            o0 = OWN_TILES.index(tt) * 512 if own else None
            hs = tt % 2
            hTt = hT[hs]
            ws = ii % 3
            wt = wg[ws]
            if g in (0, 1):
                fm = [(ci, qaT, g * 4 + ci, sA, True) for ci in range(4)]
            elif g == 2:
                fm = [(ci, kaT, ci, None, False) for ci in range(2)]
            elif g in (3, 4):
                fm = [(ci, qbT, (g - 3) * 4 + ci, 0.125, True) for ci in range(4)]
            elif g in (5, 6):
                fm = [(ci, kbT, (g - 5) * 4 + ci, None, False) for ci in range(4)]
            else:
                fm = []
            for (ci, dst, hidx, scale, isq) in fm:
                b = pgi.next()
                for kc in range(NCH):
                    P.op("pe", lambda e, b=b, kc=kc, ci=ci, wt=wt, hTt=hTt: e.matmul(
                        banks[b], wt[:, kc, ci * 128:(ci + 1) * 128], hTt[:, kc, :], start=(kc == 0), stop=(kc == NCH - 1)),
                        reads=[("wg", ws)] + [("hT", hs, jj, kc // 8) for jj in range(4)], writes=[("bank", b)])
                st = sti.next()
                evac(stage[st][:], banks[b], reads=[("bank", b)], writes=[("stage", st)], scale=scale)
                if isq:
                    P.dma("pool", dst[hidx, :, o0:o0 + 512], stage[st][:], reads=[("stage", st)])
                else:
                    P.dma("pool", dst[hidx, :, tt * 512:(tt + 1) * 512], stage[st][:], reads=[("stage", st)])
            if g == 2:
                tmv = (256, 256, va, 0, 2)
            elif g in (7, 8):
                tmv = (0, 512, vb, (g - 7) * 4, 4)
            else:
                tmv = None
            if tmv is not None:
                c0, ncols, dst, h0, nh = tmv
                for j in range(4):
                    b = pgi.next()
                    for kc in range(NCH):
                        P.op("pe", lambda e, b=b, kc=kc, j=j, wt=wt, hTt=hTt, c0=c0, ncols=ncols: e.matmul(
                            banks[b][:, 0:ncols], hTt[:, kc, j * 128:(j + 1) * 128], wt[:, kc, c0:c0 + ncols],
                            start=(kc == 0), stop=(kc == NCH - 1)),
                            reads=[("wg", ws), ("hT", hs, j, kc // 8)], writes=[("bank", b)])
                    st = sti.next()
                    evac(stage[st][:, 0:ncols], banks[b][:, 0:ncols], reads=[("bank", b)], writes=[("stage", st)])
                    tok = tt * 512 + j * 128
                    P.dma("pool", dst[h0:h0 + nh, tok:tok + 128, :].rearrange("h t d -> t h d"),
                          stage[st][:, 0:ncols].rearrange("p (h d) -> p h d", h=nh), reads=[("stage", st)])
            if tt + 1 < NT1:
                if gi < 4:
                    pro_front(tt + 1, gi)
                if 1 <= gi <= 4:
                    pro_back(tt + 1, gi - 1)
            if ii % 4 == 3:
                emit_late_cast()
        while late_casts:
            emit_late_cast()
        P.barrier()

    if "p2" in phases:
        P.arena_reset()
        kA = P.asb("kA", [128, 8192], BF16)
        vA = P.asb("vA", [128, 64, 128], BF16)
        qA = [P.asb("qA", [128, 4096], BF16) for _ in range(2)]
        sAb = [P.asb("sAb", [128, 3, 512], F32) for _ in range(2)]
        Eb = [P.asb("Eb", [128, 3, 512], BF16) for _ in range(2)]
        zs = P.asb("zs", [1, 512], F32)
        zh = P.asb("zh", [1, 512], BF16)
        zl = P.asb("zl", [1, 512], BF16)
        bz = P.asb("bz", [128, 512], F32)
        zrow = P.asb("zrow", [1, 512], BF16)
        stg = [P.asb("stgA", [128, 512], BF16) for _ in range(2)]
        P.op("pool", lambda e: e.memset(zrow[:], 0.0), writes=["zrow"])
        PIECES = {-1: (0, 0, 0, 1, 2), 1: (0, 128, 0, 3, 0), 0: (1, 0, 0, 2, 1), 3: (1, 256, 2, 2, 0), 2: (2, 0, 1, 3, 0), 4: (2, 384, 3, 1, 0)}
        hq = 0
        stcl = [0]
        tl = 0
        for ji, job in enumerate(JOBS):
            tok0, n_own, n_tot, own0 = job["tok0"], job["n_own"], job["n_tot"], job["own0"]
            nq = n_own // 128
            nqt = n_own // 512
            nb = n_tot // 128
            for kv in range(2):
                P.dma("sp", kA[:, 0:n_tot], kaT[kv, :, tok0:tok0 + n_tot], writes=["kA"])
                P.dma("sp", vA[:, 0:nb, :], va[kv, tok0:tok0 + n_tot, :].rearrange("(b p) d -> p b d", p=128), writes=["vA"])
                for hh in range(4):
                    h = kv * 4 + hh
                    qs = hq % 2
                    hq += 1
                    qAt = qA[qs]
                    P.dma("sp", qAt[:, 0:n_own], qaT[h, :, own0:own0 + n_own], writes=[("qA", qs)])

                    def kb_of(qt, r):
                        kb = qt * 4 + r
                        if kb < 0:
                            return nq + 1
                        if kb >= nq:
                            return nq
                        return kb

                    def emit_S(qt):
                        for r in range(-1, 5):
                            bnk, c0, qb0, nqb, j0 = PIECES[r]
                            kb = kb_of(qt, r)
                            P.op("pe", lambda e: e.matmul(banks[bnk][:, c0:c0 + nqb * 128], kA[:, kb * 128:(kb + 1) * 128],
                                                          qAt[:, (qt * 4 + qb0) * 128:(qt * 4 + qb0 + nqb) * 128], start=True, stop=True),
                                 reads=["kA", ("qA", qs)], writes=[("bank", bnk)])

                    def emit_epi(qt, ps):
                        pO = banks[3 + ps]
                        pZ = banks[5 + ps]
                        P.op("dve", lambda e: e.tensor_scalar(zs[:], pZ[0:1, :], esink[0:1, h:h + 1], None, ALU.add),
                             reads=[("bank", 5 + ps), "esink"], writes=["zs"])
                        P.op("dve", lambda e: e.tensor_copy(zh[:], zs[:]), reads=["zs"], writes=["zh"])
                        P.op("dve", lambda e: e.tensor_tensor(zl[:], zs[:], zh[:], ALU.subtract), reads=["zs", "zh"], writes=["zl"])
                        P.op("pe", lambda e: e.matmul(banks[7], ones_bw[0:1, :], zh[:], start=True, stop=False),
                             reads=["ones32", "zh"], writes=[("bank", 7)])
                        P.op("pe", lambda e: e.matmul(banks[7], ones_bw[0:1, :], zl[:], start=False, stop=True),
                             reads=["ones32", "zl"], writes=[("bank", 7)])
                        P.op("act", lambda e: e.activation(bz[:], banks[7], AF.Ln), reads=[("bank", 7)], writes=["bz"])
                        P.op("act", lambda e: e.activation(bz[:], bz[:], AF.Exp, scale=-1.0), reads=["bz"], writes=["bz"])
                        st = stcl[0] % 2
                        stcl[0] += 1
                        P.op("dve", lambda e: e.tensor_tensor(stg[st][:], pO, bz[:], ALU.mult),
                             reads=[("bank", 3 + ps), "bz"], writes=[("stgA", st)])
                        P.dma("sp", attT[h, :, own0 + qt * 512:own0 + (qt + 1) * 512], stg[st][:], reads=[("stgA", st)])

                    emit_S(0)
                    for qt in range(nqt):
                        sl = tl % 2
                        ps = tl % 2
                        tl += 1
                        sAt = sAb[sl]
                        Et = Eb[sl]
                        for r in range(-1, 5):
                            bnk, c0, qb0, nqb, j0 = PIECES[r]
                            tb = tabA[:, h, j0:j0 + nqb, :].rearrange("p j q -> p (j q)")
                            P.op("dve", lambda e: e.tensor_tensor(sAt[:, bnk, c0:c0 + nqb * 128], banks[bnk][:, c0:c0 + nqb * 128], tb, ALU.add),
                                 reads=[("bank", bnk), "tabA"], writes=[("sAb", sl, bnk)])
                            if (qt == 0 and r == -1) or (qt == nqt - 1 and r == 4):
                                mc = 2 * ji + (0 if r == -1 else 1)
                                P.op("dve", lambda e: e.tensor_scalar(sAt[:, bnk, c0:c0 + 128], sAt[:, bnk, c0:c0 + 128], hmask[:, mc:mc + 1], None, ALU.add),
                                     reads=[("sAb", sl, bnk), "hmask"], writes=[("sAb", sl, bnk)])
                        for bnk in range(3):
                            P.op("act", lambda e: e.activation(Et[:, bnk, :], sAt[:, bnk, :], AF.Exp), reads=[("sAb", sl, bnk)], writes=[("Eb", sl, bnk)])
                        if qt + 1 < nqt:
                            emit_S(qt + 1)
                        if qt > 0:
                            emit_epi(qt - 1, 1 - ps)
                        P.op("pe", lambda e: e.matmul(banks[3 + ps], zrow[0:1, 0:128], zrow[0:1, :], start=True, stop=False),
                             reads=["zrow"], writes=[("bank", 3 + ps)])
                        P.op("pe", lambda e: e.matmul(banks[5 + ps][0:1, :], zrow[0:1, 0:1], zrow[0:1, :], start=True, stop=False),
                             reads=["zrow"], writes=[("bank", 5 + ps)])
                        for r in range(-1, 5):
                            bnk, c0, qb0, nqb, j0 = PIECES[r]
                            kb = kb_of(qt, r)
                            last = (r == 4)
                            P.op("pe", lambda e: e.matmul(banks[3 + ps][:, qb0 * 128:(qb0 + nqb) * 128], vA[:, kb, :], Et[:, bnk, c0:c0 + nqb * 128],
                                                          start=False, stop=last),
                                 reads=["vA", ("Eb", sl, bnk)], writes=[("bank", 3 + ps)])
                            P.op("pe", lambda e: e.matmul(banks[5 + ps][0:1, qb0 * 128:(qb0 + nqb) * 128], ones_bf[:, 0:1], Et[:, bnk, c0:c0 + nqb * 128],
                                                          start=False, stop=last),
                                 reads=["ones_bf", ("Eb", sl, bnk)], writes=[("bank", 5 + ps)])
                    emit_epi(nqt - 1, (tl - 1) % 2)
        P.barrier()

    if "p3" in phases:
        P.arena_reset()
        KT = [P.asb("KT", [128, 2, 8192], BF16) for _ in range(2)]
        VB = [P.asb("VB", [128, 64, 128], BF16) for _ in range(2)]
        QPs = [P.asb("QP", [128, 2, 4096], BF16) for _ in range(2)]
        QMs = [P.asb("QM", [128, 2, 4096], BF16) for _ in range(2)]
        Et = [P.asb("E", [128, 2, 512], BF16) for _ in range(3)]
        accz = P.asb("accz", [128, 2, 512], F32)
        O1 = P.asb("O1", [128, 512], F32)
        O2 = P.asb("O2", [128, 512], F32)
        rz = P.asb("rz3", [64, 512], F32)
        zh3 = P.asb("zh3", [64, 512], BF16)
        zl3 = P.asb("zl3", [64, 512], BF16)
        ob = P.asb("ob", [128, 512], F32)
        sq = P.asb("sq", [128, 512], F32)
        stg = [P.asb("stgB", [128, 512], BF16) for _ in range(2)]
        heads = [(ji, h) for ji in range(len(JOBS)) for h in range(8)]

        def emit_loads(idx):
            ji_, h_ = heads[idx]
            job_ = JOBS[ji_]
            tok0_, n_own_, n_tot_, own0_ = job_["tok0"], job_["n_own"], job_["n_tot"], job_["own0"]
            b_ = idx % 2
            for m in range(2):
                P.dma("sp", KT[b_][0:64, m, 0:n_tot_], kbT[h_, m * 64:(m + 1) * 64, tok0_:tok0_ + n_tot_], writes=[("KT", b_, m)])
                P.dma("pool", KT[b_][64:72, m, 0:n_tot_], kaug8[h_, :, tok0_:tok0_ + n_tot_], writes=[("KTa", b_, m)])
            P.dma("sp", VB[b_][:, 0:n_tot_ // 128, :], vb[h_, tok0_:tok0_ + n_tot_, :].rearrange("(b p) d -> p b d", p=128), writes=[("VB", b_)])
            for m in range(2):
                P.dma("sp", QPs[b_][0:64, m, 0:n_own_], qbT[h_, m * 64:(m + 1) * 64, own0_:own0_ + n_own_], writes=[("QP", b_, m)])
                P.dma("sp", QMs[b_][0:64, m, 0:n_own_], qbT[h_, m * 64:(m + 1) * 64, own0_:own0_ + n_own_], writes=[("QM", b_, m)])
                P.dma("pool", QPs[b_][64:72, m, 0:n_own_], qaugP[:, own0_:own0_ + n_own_], writes=[("QPa", b_, m)])
                P.dma("pool", QMs[b_][64:72, m, 0:n_own_], qaugM[:, own0_:own0_ + n_own_], writes=[("QMa", b_, m)])

        pending = []
        stc = 0
        emit_loads(0)
        for hidx, (ji, h) in enumerate(heads):
            job = JOBS[ji]
            tok0, n_own, n_tot, own0 = job["tok0"], job["n_own"], job["n_tot"], job["own0"]
            nqt = n_own // 512
            nb = n_tot // 128
            nbo = n_own // 128
            if True:
                kb_ = hidx % 2
                if hidx + 1 < len(heads):
                    emit_loads(hidx + 1)
                KTt = KT[kb_]
                VBt = VB[kb_]
                QP = QPs[kb_]
                QM = QMs[kb_]
                def keep(qt, kb):
                    if ALIBI_SKIP_T is None:
                        return True
                    if kb < nbo:
                        if qt * 4 <= kb < qt * 4 + 4:
                            return True
                        dmin = qt * 512 - (kb * 128 + 127) if kb < qt * 4 else kb * 128 - (qt * 512 + 511)
                    else:
                        i = kb - nbo
                        slot_d = (max(0, i - 1) if ji == 0 else i // 2) * 128
                        dmin = min(qt, nqt - 1 - qt) * 512 + slot_d + 1
                    return float(2.0 ** -(h + 1)) * dmin < ALIBI_SKIP_T

                steps = []
                for qt in range(nqt):
                    kbs = [kb for kb in range(nb) if keep(qt, kb)]
                    for pi, kb in enumerate(kbs):
                        steps.append((qt, kb, pi, len(kbs)))

                def emit_QK(si):
                    qt, kb, _pi, _nk = steps[si]
                    slot = si % 2
                    q0 = qt * 512
                    kc = slice(kb * 128, (kb + 1) * 128)
                    for m in range(2):
                        pS = pairs[slot][:, m, :]
                        bk = ("pS", slot, m)
                        rk = [("KT", kb_, m), ("KTa", kb_, m)]
                        rp = rk + [("QP", kb_, m), ("QPa", kb_, m)]
                        rm = rk + [("QM", kb_, m), ("QMa", kb_, m)]
                        if kb >= nbo or kb < qt * 4:
                            P.op("pe", lambda e: e.matmul(pS, KTt[0:72, m, kc], QP[0:72, m, q0:q0 + 512], start=True, stop=True),
                                 reads=rp, writes=[bk])
                        elif kb >= qt * 4 + 4:
                            P.op("pe", lambda e: e.matmul(pS, KTt[0:72, m, kc], QM[0:72, m, q0:q0 + 512], start=True, stop=True),
                                 reads=rm, writes=[bk])
                        else:
                            d = kb - qt * 4
                            if d > 0:
                                P.op("pe", lambda e: e.matmul(pS[:, 0:d * 128], KTt[0:72, m, kc], QM[0:72, m, q0:q0 + d * 128],
                                                              start=True, stop=True), reads=rm, writes=[bk])
                            P.op("pe", lambda e: e.matmul(pS[:, d * 128:(d + 1) * 128], KTt[0:64, m, kc],
                                                          QP[0:64, m, q0 + d * 128:q0 + (d + 1) * 128], start=True, stop=True),
                                 reads=rp, writes=[bk])
                            if d < 3:
                                P.op("pe", lambda e: e.matmul(pS[:, (d + 1) * 128:512], KTt[0:72, m, kc],
                                                              QP[0:72, m, q0 + (d + 1) * 128:q0 + 512], start=True, stop=True),
                                     reads=rp, writes=[bk])
                            P.op("dve", lambda e: e.tensor_tensor(pS[:, d * 128:(d + 1) * 128], pS[:, d * 128:(d + 1) * 128],
                                                                  dtabB[:, h, :], ALU.add),
                                 reads=[bk, "dtabB"], writes=[bk])

                emit_QK(0)
                if len(steps) > 1:
                    emit_QK(1)
                for si, (qt, kb, pi, nk) in enumerate(steps):
                    slot = si % 2
                    es = si % 3
                    P.op("act", lambda e: e.activation(Et[es][:], pairs[slot][:], AF.Exp),
                         reads=[("pS", slot, 0), ("pS", slot, 1)], writes=[("E", es)])
                    if si + 2 < len(steps):
                        emit_QK(si + 2)
                    for m in range(2):
                        P.op("pe", lambda e: e.matmul(banks[4 + m], VBt[:, kb, :], Et[es][:, m, :], start=(pi == 0), stop=(pi == nk - 1)),
                             reads=[("VB", kb_), ("E", es)], writes=[("bank", 4 + m)])
                    last_pe = (nk - 1) // 3 * 3
                    if pi % 3 == 0:
                        P.op("pe", lambda e: e.matmul(banks[6][0:32, :], ones32, Et[es][:, 0, :], start=(pi == 0), stop=(pi == last_pe),
                                                      tile_position=(0, 0)),
                             reads=["ones32", ("E", es)], writes=[("bank", 6)])
                        P.op("pe", lambda e: e.matmul(banks[6][32:64, :], ones32, Et[es][:, 1, :], start=(pi == 0), stop=(pi == last_pe),
                                                      tile_position=(0, 32)),
                             reads=["ones32", ("E", es)], writes=[("bank", 6)])
                    elif pi == 1:
                        P.op("dve", lambda e: e.tensor_copy(accz[:], Et[es][:]), reads=[("E", es)], writes=["accz"])
                    else:
                        P.op("dve", lambda e: e.tensor_tensor(accz[:], accz[:], Et[es][:], ALU.add), reads=[("E", es), "accz"], writes=["accz"])
                    if pi == nk - 1:
                        P.op("pe", lambda e: e.matmul(banks[7][0:64, :], self32[:, 0:64], accz[:, 0, :], start=True, stop=False),
                             reads=["self32", "accz"], writes=[("bank", 7)])
                        P.op("pe", lambda e: e.matmul(banks[7][0:64, :], self32[:, 64:128], accz[:, 1, :], start=False, stop=True),
                             reads=["self32", "accz"], writes=[("bank", 7)])
                        P.op("dve", lambda e: e.tensor_copy(O1[:], banks[4]), reads=[("bank", 4)], writes=["O1"])
                        P.op("dve", lambda e: e.tensor_copy(O2[:], banks[5]), reads=[("bank", 5)], writes=["O2"])
                        P.op("dve", lambda e: e.tensor_copy(rz[0:33, :], banks[7][0:33, :]), reads=[("bank", 7)], writes=["rz3"])
                        P.op("dve", lambda e: e.tensor_tensor(rz[0:33, :], rz[0:33, :], banks[6][0:33, :], ALU.add), reads=[("bank", 6), "rz3"], writes=["rz3"])
                        P.op("dve", lambda e: e.tensor_copy(zh3[0:33, :], rz[0:33, :]), reads=["rz3"], writes=["zh3"])
                        P.op("dve", lambda e: e.tensor_tensor(zl3[0:33, :], rz[0:33, :], zh3[0:33, :], ALU.subtract), reads=["rz3", "zh3"], writes=["zl3"])
                        st = stc % 2
                        stc += 1

                        def stage2():
                            P.op("pe", lambda e: e.matmul(banks[7], ones_bw[0:1, :], zh3[0:1, :], start=True, stop=False),
                                 reads=["ones32", "zh3"], writes=[("bank", 7)])
                            P.op("pe", lambda e: e.matmul(banks[7], ones_bw[0:1, :], zl3[0:1, :], start=False, stop=True),
                                 reads=["ones32", "zl3"], writes=[("bank", 7)])
                            P.op("act", lambda e: e.activation(ob[:], banks[7], AF.Ln), reads=[("bank", 7)], writes=["ob"])
                            P.op("act", lambda e: e.activation(ob[:], ob[:], AF.Exp, scale=-1.0), reads=["ob"], writes=["ob"])

                        def stage3():
                            P.op("dve", lambda e: e.tensor_tensor(O1[:], O1[:], ob[:], ALU.mult), reads=["O1", "ob"], writes=["O1"])
                            P.op("pe", lambda e: e.matmul(banks[7], ones_bw[32:33, :], zh3[32:33, :], start=True, stop=False),
                                 reads=["ones32", "zh3"], writes=[("bank", 7)])
                            P.op("pe", lambda e: e.matmul(banks[7], ones_bw[32:33, :], zl3[32:33, :], start=False, stop=True),
                                 reads=["ones32", "zl3"], writes=[("bank", 7)])
                            P.op("act", lambda e: e.activation(sq[:], banks[7], AF.Ln), reads=[("bank", 7)], writes=["sq"])
                            P.op("act", lambda e: e.activation(sq[:], sq[:], AF.Exp, scale=-1.0), reads=["sq"], writes=["sq"])

                        def stage4():
                            P.op("dve", lambda e: e.tensor_tensor(O2[:], O2[:], sq[:], ALU.mult), reads=["O2", "sq"], writes=["O2"])
                            P.op("dve", lambda e: e.scalar_tensor_tensor(out=ob[:], in0=O2[:], scalar=neglam[:, 0:1], in1=O1[:],
                                                                         op0=ALU.mult, op1=ALU.add),
                                 reads=["O1", "O2", "neglam"], writes=["ob"])
                            P.op("pool", lambda e: e.tensor_tensor(sq[:], ob[:], ob[:], ALU.mult), reads=["ob"], writes=["sq"])

                        def stage5(h=h, qt=qt, own0=own0, st=st):
                            P.op("pe", lambda e: e.matmul(banks[7], ones_f[:], sq[:], start=True, stop=True),
                                 reads=["ones_f", "sq"], writes=[("bank", 7)])
                            P.op("act", lambda e: e.activation(sq[:], banks[7], AF.Ln, scale=1.0 / 128, bias=epsb[:]),
                                 reads=[("bank", 7), "epsb"], writes=["sq"])
                            P.op("act", lambda e: e.activation(sq[:], sq[:], AF.Exp, scale=-0.5), reads=["sq"], writes=["sq"])

                        def stage6(h=h, qt=qt, own0=own0, st=st):
                            P.op("dve", lambda e: e.scalar_tensor_tensor(out=stg[st][:], in0=ob[:], scalar=gsub[:, 0:1], in1=sq[:],
                                                                         op0=ALU.mult, op1=ALU.mult),
                                 reads=["ob", "sq", "gsub"], writes=[("stgB", st)])
                            P.dma("pool", attT[8 + h, :, own0 + qt * 512:own0 + (qt + 1) * 512], stg[st][:], reads=[("stgB", st)])

                        pending.extend([stage2, stage3, stage4, stage5, stage6])
                    elif pending:
                        pending.pop(0)()
        while pending:
            pending.pop(0)()
        P.barrier()

    while late_casts:
        emit_late_cast()
    if "p4" in phases:
        P.arena_reset()
        gtm = P.asb("gtab_mlp", [128, D], F32)
        gtf = P.asb("gtab_fin", [128, D], F32)
        x1 = P.asb("x1", [128, 4, D], F32)
        aT = P.asb("aT", [128, NCH, 512], BF16)
        h2T_off = (P.arena_ptr + 63) // 64 * 64
        h2T = P.asb("h2T", [128, NCH, 512], BF16)
        yo = [nc.alloc_sbuf_tensor_at("yo%d" % i, [128, D], F32, offset=h2T_off + i * D * 4) for i in range(2)]
        uT = P.asb("uT", [128, 64, 512], BF16)
        wg = [P.asb("wg4", [128, NCH, 512], BF16) for _ in range(2)]
        hb = [P.asb("hb4", [128, D], BF16) for _ in range(1)]
        junk = P.asb("junk4", [128, D], BF16)
        rl_off = (P.arena_ptr + 63) // 64 * 64
        rl = [P.asb("rl", [128, 512], F32) for _ in range(2)]
        hb.append(nc.alloc_sbuf_tensor_at("hb4_alias", [128, D], BF16, offset=rl_off))
        P.dma("sp", gtm[:], g_mlp[0].partition_broadcast(128), writes=["gtm"])
        P.dma("sp", gtf[:], g_fin[0].partition_broadcast(128), writes=["gtf"])
        wgi = RR(range(2))
        pgi = RR([2, 3, 4, 5, 6, 7])
        cnt4 = 0
        def load_x1(ti, j):
            tok = OWN_TILES[ti] * 512 + j * 128
            P.dma("pool", x1[:, j, :], xc[tok:tok + 128, :], writes=[("x1", j, n) for n in range(4)])

        def load_aT(ti):
            P.dma("sp", aT[:], attT[:, :, ti * 512:(ti + 1) * 512].rearrange("c p t -> p c t"), writes=["aT"])

        load_aT(0)
        for j in range(4):
            load_x1(0, j)
        for ti, tt in enumerate(OWN_TILES):
            o0 = ti * 512
            tok = tt * 512
            for nt in range(4):
                ws = wgi.next()
                wt = wg[ws]
                P.dma("sp", wt[:], w_out_bf[nt],
                      reads=[("w_out_bf", nt)], writes=[("wg", ws)])
                for j in range(4):
                    b = pgi.next()
                    for kc in range(NCH):
                        P.op("pe", lambda e, b=b, kc=kc, j=j, wt=wt: e.matmul(
                            banks[b], aT[:, kc, j * 128:(j + 1) * 128], wt[:, kc, :], start=(kc == 0), stop=(kc == NCH - 1)),
                            reads=["aT", ("wg", ws)], writes=[("bank", b)])
                    P.op("dve", lambda e, b=b, j=j, nt=nt: e.tensor_tensor(
                        x1[:, j, nt * 512:(nt + 1) * 512], banks[b], x1[:, j, nt * 512:(nt + 1) * 512], ALU.add),
                        reads=[("bank", b), ("x1", j, nt)], writes=[("x1", j, nt)])
            if ti + 1 < len(OWN_TILES):
                load_aT(ti + 1)
            hbk = [("hb", 0), ("rl", 0)]
            for j in range(5):
                if j < 4:
                    cnt4 += 1
                    hk = [hbk[j % 2]] + ([("rl", 1)] if j % 2 == 1 else [])
                    norm_front(x1[:, j, :], [("x1", j, n) for n in range(4)], gtm, "gtm", hb[j % 2], hk[0], cnt4 % 4, junk)
                    if j % 2 == 1:
                        P.last_writer[("rl", 1)] = P.last_writer[("rl", 0)]
                        P.readers[("rl", 1)] = ({}, [])
                if j >= 1:
                    norm_back(hb[(j - 1) % 2], hbk[(j - 1) % 2], h2T, lambda jj, half: ("h2T", jj, half), j - 1)
            for fg in range(16):
                ws = wgi.next()
                wt = wg[ws]
                P.dma("sp", wt[:], w_up_bf[fg],
                      reads=[("w_up_bf", fg)], writes=[("wg", ws)])
                for ci in range(4):
                    ffc = fg * 4 + ci
                    b = pgi.next()
                    for kc in range(NCH):
                        P.op("pe", lambda e, b=b, kc=kc, ci=ci, wt=wt: e.matmul(
                            banks[b], wt[:, kc, ci * 128:(ci + 1) * 128], h2T[:, kc, :], start=(kc == 0), stop=(kc == NCH - 1)),
                            reads=[("wg", ws)] + [("h2T", jj, kc // 8) for jj in range(4)], writes=[("bank", b)])
                    rs = ffc % 2
                    P.op("act", lambda e, b=b, rs=rs: e.activation(rl[rs][:], banks[b], AF.Relu), reads=[("bank", b)], writes=[("rl", rs)])
                    P.op("pool", lambda e, rs=rs, ffc=ffc: e.tensor_tensor(uT[:, ffc, :], rl[rs][:], rl[rs][:], ALU.mult),
                         reads=[("rl", rs)], writes=[("uT", ffc)])
            for nt in range(4):
                for fg in range(4):
                    ws = wgi.next()
                    wt = wg[ws]
                    P.dma("sp", wt[:], w_down_bf[nt * 4 + fg],
                          reads=[("w_down_bf", nt, fg)], writes=[("wg", ws)])
                    for j in range(4):
                        for fc in range(16):
                            ffc = fg * 16 + fc
                            P.op("pe", lambda e, j=j, fc=fc, ffc=ffc, wt=wt, fg=fg: e.matmul(
                                banks[4 + j], uT[:, ffc, j * 128:(j + 1) * 128], wt[:, fc, :],
                                start=(fg == 0 and fc == 0), stop=(fg == 3 and fc == 15)),
                                reads=[("uT", ffc), ("wg", ws)], writes=[("bank", 4 + j)])
                for j in range(4):
                    P.op("dve", lambda e, j=j, nt=nt: e.tensor_tensor(
                        x1[:, j, nt * 512:(nt + 1) * 512], banks[4 + j], x1[:, j, nt * 512:(nt + 1) * 512], ALU.add),
                        reads=[("bank", 4 + j), ("x1", j, nt)], writes=[("x1", j, nt)])
            for j in range(4):
                cnt4 += 1
                si = cnt4 % 4
                ys = cnt4 % 2
                xk = [("x1", j, n) for n in range(4)]
                P.op("act", lambda e, j=j, si=si: e.activation(junk[:], x1[:, j, :], AF.Square, accum_out=ss[:, si:si + 1]),
                     reads=xk, writes=["junk", ("ss", si)])
                P.op("act", lambda e, si=si: e.activation(rstd[:, si:si + 1], ss[:, si:si + 1], AF.Ln, scale=1.0 / D, bias=epsb[:]),
                     reads=[("ss", si), "epsb"], writes=[("rstd", si)])
                P.op("act", lambda e, si=si: e.activation(rstd[:, si:si + 1], rstd[:, si:si + 1], AF.Exp, scale=-0.5),
                     reads=[("rstd", si)], writes=[("rstd", si)])
                yk = [("h2T", jj, ys) for jj in range(4)]
                P.op("dve", lambda e, j=j, si=si, ys=ys: e.scalar_tensor_tensor(out=yo[ys][:], in0=x1[:, j, :], scalar=rstd[:, si:si + 1],
                                                                                in1=gtf[:], op0=ALU.mult, op1=ALU.mult),
                     reads=xk + [("rstd", si), "gtf"], writes=yk)
                P.dma("pool", y[o0 + j * 128:o0 + (j + 1) * 128, :], yo[ys][:], reads=yk)
                if ti + 1 < len(OWN_TILES):
                    load_x1(ti + 1, j)

    P.emit()
    ncd.__exit__(None, None, None)
    lp.__exit__(None, None, None)
    return nc, P


def _slopes():
    i = np.arange(1, 17, dtype=np.float32)
    s = np.exp2(np.float32(-8.0 / 16.0) * i).astype(np.float32)
    return s[0::2], s[1::2]


def _order_others(after, before):
    a = [after[i * 128:(i + 1) * 128] for i in range(len(after) // 128)]
    b = [before[i * 128:(i + 1) * 128] for i in range(len(before) // 128)][::-1]
    out = []
    if a and b:
        while a or b:
            if a:
                out.append(a.pop(0))
            if b:
                out.append(b.pop(0))
    elif a:
        out = a
    else:
        out = [b[1], b[0]] + b[2:]
    return np.concatenate(out)


def _core_layout(c):
    hf = c % 2
    so = np.arange(hf * 4096, hf * 4096 + 4096)
    sa = np.arange((hf + 1) * 4096, 8192)
    sbf = np.arange(0, hf * 4096)
    qt = c % 4
    po = np.arange(qt * 1024, qt * 1024 + 1024)
    pa = np.arange(qt * 1024 + 1024, 4096)
    pb = np.arange(0, qt * 1024)
    return ((c // 2, np.concatenate([so, _order_others(sa, sbf)]), hf * 4096, 0),
            (c // 4, np.concatenate([po, _order_others(pa, pb)]), qt * 1024, 0))


def _pos_tables(c):
    sl_a, sl_b = _slopes()
    (s, sord, sstart, s_na), (p, pord, pstart, p_na) = _core_layout(c)
    kaug8 = np.zeros((8, 8, NTOK), np.float32)
    qaugP = np.zeros((8, NOWN), np.float32)
    qaugM = np.zeros((8, NOWN), np.float32)
    for (order, start, n_own, n_after, tok0, own0) in ((sord, sstart, 4096, s_na, 0, 0), (pord, pstart, 1024, p_na, 8192, 4096)):
        rel = (order - start).astype(np.int64)
        hi = np.floor_divide(rel, 128)
        lo = rel - 128 * hi
        n = len(order)
        sig = np.where(rel < 0, 1.0, -1.0).astype(np.float32)
        sig[:n_own] = 1.0
        for h in range(8):
            sl = sl_b[h]
            plus = np.stack([sig * sl * 128.0 * hi, sig * sl * lo, -sig * sl * np.ones(n), -sig * sl * np.ones(n)]).astype(np.float32)
            kaug8[h, 0:4, tok0:tok0 + n] = plus
            kaug8[h, 4:8, tok0:tok0 + n_own] = -plus[:, 0:n_own]
        qa = np.stack([np.ones(n_own), np.ones(n_own), 128.0 * hi[:n_own], lo[:n_own]]).astype(np.float32)
        qaugP[0:4, own0:own0 + n_own] = qa
        qaugM[4:8, own0:own0 + n_own] = qa
    hf, qt = c % 2, c % 4
    hm = np.zeros((128, 4), np.float32)
    hm[:, 0] = 0.0 if hf == 1 else NEG
    hm[:, 1] = 0.0 if hf == 0 else NEG
    hm[:, 2] = 0.0 if qt > 0 else NEG
    hm[:, 3] = 0.0 if qt < 3 else NEG
    return kaug8, qaugP, qaugM, hm


def _static_tables():
    sl_a, sl_b = _slopes()
    k = np.arange(128)[:, None]
    q = np.arange(128)[None, :]
    dtabB = np.stack([-sl_b[h] * np.abs(q - k).astype(np.float32) for h in range(8)], axis=1).astype(np.float32)
    tabA = np.zeros((128, 8, 3, 128), np.float32)
    dists = [128 + k - q, np.abs(q - k), 128 + q - k]
    for h in range(8):
        for r in range(3):
            dd = dists[r].astype(np.float32)
            tabA[:, h, r, :] = np.where(dd <= 128, -sl_a[h] * dd, NEG)
    return dtabB.reshape(128, -1), tabA.reshape(128, -1)


_CACHE = {}


def kernel(x_prompt, x_sample, norm_attn_g, w_in, sink_logits, lambda_q1, lambda_k1, lambda_q2, lambda_k2,
           diff_subln_g, w_out, norm_mlp_g, w_up, w_down, norm_final_g):
    f = lambda a: np.ascontiguousarray(np.asarray(a, dtype=np.float32))
    x_prompt, x_sample = f(x_prompt), f(x_sample)
    if "nc" not in _CACHE:
        _CACHE["nc"] = build()[0]
    nc = _CACHE["nc"]
    dtabB, tabA = _static_tables()
    shared = dict(w_in=f(w_in)[0], w_out=f(w_out)[0], w_up=f(w_up)[0], w_down=f(w_down)[0],
                  g_attn=f(norm_attn_g).reshape(1, D), g_mlp=f(norm_mlp_g).reshape(1, D), g_fin=f(norm_final_g).reshape(1, D),
                  sink=f(sink_logits).reshape(1, 8), lq1=f(lambda_q1).reshape(1, 64), lk1=f(lambda_k1).reshape(1, 64),
                  lq2=f(lambda_q2).reshape(1, 64), lk2=f(lambda_k2).reshape(1, 64), subg=f(diff_subln_g).reshape(1, 128),
                  dtabB=dtabB, tabA=tabA)
    in_maps = []
    lay = []
    for c in range(8):
        (s, sord, sstart, _), (p, pord, pstart, _) = _core_layout(c)
        kaug8, qaugP, qaugM, hm = _pos_tables(c)
        xcore = np.concatenate([x_sample[s][sord], x_prompt[p][pord]], axis=0)
        m = dict(shared)
        m.update(xc=np.ascontiguousarray(xcore), kaug8=kaug8, qaugP=qaugP, qaugM=qaugM, hmask=hm)
        in_maps.append(m)
        lay.append((s, sstart, p, pstart))
    res = run_bass_kernel_spmd(nc, in_maps, core_ids=list(range(8)))
    y_prompt = np.zeros((2, 4096, D), np.float32)
    y_sample = np.zeros((4, 8192, D), np.float32)
    for c in range(8):
        yc = np.asarray(res.results[c]["y"])
        s, sstart, p, pstart = lay[c]
        y_sample[s, sstart:sstart + 4096] = yc[0:4096]
        y_prompt[p, pstart:pstart + 1024] = yc[4096:5120]
    return (y_prompt, y_sample)
```
